# Optimizing a Trainium2 kernel written in Bass

```python
import jax, jax.numpy as jnp
from jax import lax
import numpy as np

D_MODEL = 1024
BATCH = 8
SEQ = 8192
DEPTH = 1
DEC_BATCH = 2
DEC_SEQ = 8192
PAST_LEN = 128

GRID_W = 64
HEAD_DIM = 64
NA_HEADS = 8
NA_WIDTH = NA_HEADS * HEAD_DIM
NA_KH_MAX = 8
NA_KW = 16
SW_HEADS = 8
SW_KV_HEADS = 2
SW_GROUP = SW_HEADS // SW_KV_HEADS
SW_WIDTH = SW_HEADS * HEAD_DIM
SW_KV_WIDTH = SW_KV_HEADS * HEAD_DIM
SW_WINDOW = 128
SW_BLOCK = 128
RMS_EPS = 1e-6
IN_SIZES = (NA_WIDTH, NA_WIDTH, NA_WIDTH, NA_WIDTH,
            SW_WIDTH, SW_KV_WIDTH, SW_KV_WIDTH, SW_WIDTH,
            D_MODEL, D_MODEL)
D_IN = 4 * NA_WIDTH + 2 * SW_WIDTH + 2 * SW_KV_WIDTH + 2 * D_MODEL

kernel_name = "hybrid_natten_swa_gated_encoder"


def rms_norm(x, g):
    xf = x.astype(jnp.float32)
    y = xf * lax.rsqrt(jnp.mean(xf * xf, axis=-1, keepdims=True) + RMS_EPS)
    return (y * g.astype(jnp.float32)).astype(x.dtype)


def alibi_slopes(n):
    return jnp.asarray(2.0 ** (-8.0 * (np.arange(n) + 1) / n), dtype=jnp.float32)


def neighbourhood_attention(q, k, v, rpb):
    B, S, H, Dh = q.shape
    rows = S // GRID_W
    kh = min(NA_KH_MAX, rows)
    scale = Dh ** -0.5
    qg = q.reshape(B, rows, GRID_W, H, Dh)
    kg = k.reshape(B, rows, GRID_W, H, Dh)
    vg = v.reshape(B, rows, GRID_W, H, Dh)
    col = np.arange(GRID_W)
    col_start = np.clip(col - NA_KW // 2, 0, GRID_W - NA_KW)
    col_idx = col_start[:, None] + np.arange(NA_KW)[None, :]
    dc = col_idx - col[:, None] + NA_KW - 1
    rpb_c = rpb.astype(jnp.float32)[:, :, dc]

    def one_row(r):
        r0 = jnp.clip(r - kh // 2, 0, rows - kh)
        k_rows = lax.dynamic_slice_in_dim(kg, r0, kh, axis=1)
        v_rows = lax.dynamic_slice_in_dim(vg, r0, kh, axis=1)
        k_win = k_rows[:, :, col_idx]
        v_win = v_rows[:, :, col_idx]
        q_row = lax.dynamic_index_in_dim(qg, r, axis=1, keepdims=False)
        s = jnp.einsum('bwhd,biwjhd->bhwij', q_row, k_win,
                       preferred_element_type=jnp.float32) * scale
        dr = r0 + jnp.arange(kh) - r + NA_KH_MAX - 1
        bias = jnp.take(rpb_c, dr, axis=1).transpose(0, 2, 1, 3)
        s = s + bias[None]
        p = jax.nn.softmax(s.reshape(B, H, GRID_W, kh * NA_KW), axis=-1)
        p = p.reshape(B, H, GRID_W, kh, NA_KW).astype(v.dtype)
        return jnp.einsum('bhwij,biwjhd->bwhd', p, v_win)

    out = lax.map(one_row, jnp.arange(rows))
    return out.transpose(1, 0, 2, 3, 4).reshape(B, S, H, Dh)


def sliding_window_attention(q, k, v, sink):
    B, S, H, Dh = q.shape
    kv = k.shape[2]
    g = H // kv
    nb = S // SW_BLOCK
    scale = Dh ** -0.5
    qb = q.reshape(B, nb, SW_BLOCK, kv, g, Dh)
    pad = ((0, 0), (SW_BLOCK, SW_BLOCK), (0, 0), (0, 0))
    kp = jnp.pad(k, pad).reshape(B, nb + 2, SW_BLOCK, kv, Dh)
    vp = jnp.pad(v, pad).reshape(B, nb + 2, SW_BLOCK, kv, Dh)
    kb = jnp.concatenate([kp[:, :-2], kp[:, 1:-1], kp[:, 2:]], axis=2)
    vb = jnp.concatenate([vp[:, :-2], vp[:, 1:-1], vp[:, 2:]], axis=2)
    q_pos = jnp.arange(SW_BLOCK)
    k_pos = jnp.arange(3 * SW_BLOCK) - SW_BLOCK
    dist = jnp.abs(k_pos[None, :] - q_pos[:, None])
    abs_k = jnp.arange(nb)[:, None] * SW_BLOCK + k_pos[None, :]
    valid = (dist[None] <= SW_WINDOW) & ((abs_k >= 0) & (abs_k < S))[:, None, :]
    slopes = alibi_slopes(H).reshape(kv, g)
    s = jnp.einsum('bnqkgd,bnskd->bnkgqs', qb, kb,
                   preferred_element_type=jnp.float32) * scale
    s = s - slopes[:, :, None, None] * dist.astype(jnp.float32)
    s = jnp.where(valid[None, :, None, None], s, -jnp.inf)
    sink_kg = sink.astype(jnp.float32).reshape(kv, g)[None, None, :, :, None, None]
    m = jnp.maximum(jnp.max(s, axis=-1, keepdims=True), sink_kg)
    e = jnp.exp(s - m)
    p = e / (jnp.sum(e, axis=-1, keepdims=True) + jnp.exp(sink_kg - m))
    o = jnp.einsum('bnkgqs,bnskd->bnqkgd', p.astype(v.dtype), vb)
    return o.reshape(B, S, H, Dh)


def encoder_layer(x, c, w_ada, b_ada, g_pre, g_post, w_in, na_rpb, sw_sink, w_pa, w_pb, w_out):
    B, S, D = x.shape
    ada = jnp.einsum('bd,de->be', jax.nn.silu(c), w_ada) + b_ada
    shift, scale, gate = jnp.split(ada[:, None, :], 3, axis=-1)
    h = rms_norm(x, g_pre) * (1 + scale) + shift
    proj = jnp.einsum('bsd,de->bse', h, w_in)
    cuts = [int(v) for v in np.cumsum(IN_SIZES)[:-1]]
    qa, ka, va, za, qb, kb, vb, zb, ga, gb = jnp.split(proj, cuts, axis=-1)
    oa = neighbourhood_attention(qa.reshape(B, S, NA_HEADS, HEAD_DIM),
                                 ka.reshape(B, S, NA_HEADS, HEAD_DIM),
                                 va.reshape(B, S, NA_HEADS, HEAD_DIM), na_rpb)
    oa = oa.reshape(B, S, NA_WIDTH) * jax.nn.silu(za)
    ob = sliding_window_attention(qb.reshape(B, S, SW_HEADS, HEAD_DIM),
                                  kb.reshape(B, S, SW_KV_HEADS, HEAD_DIM),
                                  vb.reshape(B, S, SW_KV_HEADS, HEAD_DIM), sw_sink)
    ob = ob.reshape(B, S, SW_WIDTH) * jax.nn.silu(zb)
    merged = (jax.nn.sigmoid(ga) * jnp.einsum('bse,ed->bsd', oa, w_pa)
              + jax.nn.sigmoid(gb) * jnp.einsum('bse,ed->bsd', ob, w_pb))
    y = jnp.einsum('bsd,de->bse', merged, w_out)
    return x + gate * rms_norm(y, g_post)


def setup_inputs(seed: int = 0) -> dict:
    key = jax.random.key(seed)
    ks = jax.random.split(key, 16)
    f32 = jnp.float32

    def nrm(k, shape, s):
        return jax.random.normal(k, shape, dtype=f32) * s

    return {
        "x_prompt": nrm(ks[0], (BATCH, SEQ, D_MODEL), 1.0),
        "x_sample": nrm(ks[1], (DEC_BATCH, DEC_SEQ, D_MODEL), 1.0),
        "c_prompt": nrm(ks[2], (BATCH, D_MODEL), 1.0),
        "c_sample": nrm(ks[3], (DEC_BATCH, D_MODEL), 1.0),
        "w_ada": nrm(ks[4], (DEPTH, D_MODEL, 3 * D_MODEL), 0.5 * D_MODEL ** -0.5),
        "b_ada": nrm(ks[5], (DEPTH, 3 * D_MODEL), 0.01),
        "g_pre": 1.0 + nrm(ks[6], (DEPTH, D_MODEL), 0.01),
        "g_post": 1.0 + nrm(ks[7], (DEPTH, D_MODEL), 0.01),
        "w_in": nrm(ks[8], (DEPTH, D_MODEL, D_IN), D_MODEL ** -0.5),
        "na_rpb": nrm(ks[9], (DEPTH, NA_HEADS, 2 * NA_KH_MAX - 1, 2 * NA_KW - 1), 0.1),
        "sw_sink": nrm(ks[10], (DEPTH, SW_HEADS), 0.5),
        "w_pa": nrm(ks[11], (DEPTH, NA_WIDTH, D_MODEL), NA_WIDTH ** -0.5),
        "w_pb": nrm(ks[12], (DEPTH, SW_WIDTH, D_MODEL), SW_WIDTH ** -0.5),
        "w_out": nrm(ks[13], (DEPTH, D_MODEL, D_MODEL), D_MODEL ** -0.5),
    }


def reference(x_prompt, x_sample, c_prompt, c_sample, w_ada, b_ada, g_pre, g_post,
              w_in, na_rpb, sw_sink, w_pa, w_pb, w_out):
    y_prompt = x_prompt
    y_sample = x_sample
    for l in range(DEPTH):
        y_prompt = encoder_layer(y_prompt, c_prompt, w_ada[l], b_ada[l], g_pre[l], g_post[l],
                                 w_in[l], na_rpb[l], sw_sink[l], w_pa[l], w_pb[l], w_out[l])
        y_sample = encoder_layer(y_sample, c_sample, w_ada[l], b_ada[l], g_pre[l], g_post[l],
                                 w_in[l], na_rpb[l], sw_sink[l], w_pa[l], w_pb[l], w_out[l])
    return (y_prompt, y_sample)
```

```python
import numpy as np
from contextlib import ExitStack
import concourse.bass as bass
import concourse.mybir as mybir
from concourse.bass_utils import run_bass_kernel_spmd

F32 = mybir.dt.float32
BF16 = mybir.dt.bfloat16
AF = mybir.ActivationFunctionType
ALU = mybir.AluOpType

D = 1024
D_IN = 5376
NCORES = 8
RING = 12
C_QA, C_KA, C_VA, C_ZA, C_QB, C_KB, C_VB, C_ZB, C_GA, C_GB = 0, 512, 1024, 1536, 2048, 2560, 2688, 2816, 3328, 4352
U_OFF = {"QA": 0, "KA": 512, "QB": 1024, "KVB": 1536, "VA": 1792, "ZA": 2304, "ZB": 2816,
         "H0": 0, "H1": 0, "H2": 0, "H3": 0, "H4": 0, "H5": 0, "H6": 0, "H7": 0, "OUT0": 0, "OUT1": 0}
NW = 3
LOOK = 3
NE = 4
NEG = -30000.0
NX = 4
NPT = 5
U_W = {k: 512 for k in U_OFF}
U_W["KVB"] = 256
for _h in range(8):
    U_W["H%d" % _h] = 384
HEAD_ORDER = [0, 2, 4, 6, 1, 3, 5, 7]
T_F = {-3: 0, -2: 1, -1: 2, 0: 3, 1: 4, 2: 5, 3: 6}
T_IM2, T_IP2 = 7, 8
NTN = 9


class Sem:
    def __init__(self, h):
        self.h = h
        self.v = 0


class Eng:
    def __init__(self, name, sem, self_sync):
        self.name = name
        self.sem = sem
        self.seen = {}
        self.prog = []
        self.self_sync = self_sync

    def wait(self, ev):
        s, v = ev
        if s is self.sem and not self.self_sync:
            return
        if self.seen.get(id(s), 0) >= v:
            return
        self.seen[id(s)] = v
        self.prog.append(("w", s.h, v))


class Buf:
    def __init__(self, name=""):
        self.name = name
        self.w = None
        self.r = {}


def op(eng, fn, reads=(), writes=(), sem=None, inc=1, waw=True):
    for b in reads:
        if b.w is not None:
            eng.wait(b.w)
    for b in writes:
        if b.w is not None and waw:
            eng.wait(b.w)
        for s, v in b.r.values():
            eng.wait((s, v))
    s = sem if sem is not None else eng.sem
    s.v += inc
    ev = (s, s.v)
    eng.prog.append(("i", fn, s.h, inc))
    for b in reads:
        b.r[id(s)] = ev
    for b in writes:
        b.w = ev
        b.r = {}
    return ev


def I(meth, *a, **kw):
    return lambda e: getattr(e, meth)(*a, **kw)


def build(n_tb, debug=False):
    assert n_tb % 4 == 0
    n_own = n_tb * 4
    nc = bass.Bass("TRN2", target_bir_lowering=False)
    es = ExitStack()

    def din(name, shape, dt=F32):
        return nc.dram_tensor(name, list(shape), dt, kind="ExternalInput").ap()

    xe = din("xe", [(n_own + 4) * 128, D])
    csT = din("csT", [128, 8, 8])
    flg = din("flg", [128, 8])
    w_ada = din("w_ada", [D, 3 * D])
    b_adaT = din("b_adaT", [128, 16])
    b_gate8 = din("b_gate8", [8, D])
    g_preT = din("g_preT", [128, 8])
    g_post8 = din("g_post8", [8, D])
    w_in = din("w_in", [D, D_IN])
    w_pa = din("w_pa", [512, D])
    w_pb = din("w_pb", [512, D])
    w_out = din("w_out", [D, D])
    sink = din("sink", [128, 8])
    rpbx = din("rpbx", [128, 7, 1024])
    cmask = din("cmask", [128, 1024])
    rmask = din("rmask", [128, 2, 1024])
    swd = din("swd", [128, 3, 128])
    swm = din("swm", [128, 3, 1024])
    ident = din("ident", [128, 128])
    y = nc.dram_tensor("y", [n_own * 128, D], F32, kind="ExternalOutput").ap()
    win_bf = nc.dram_tensor("win_bf", [len(U_OFF), 128, 8, 512], BF16).ap()

    def sb(name, shape, dt=F32):
        return es.enter_context(nc.sbuf_tensor(name, list(shape), dt))

    def ps(name, shape, dt=F32):
        return es.enter_context(nc.psum_tensor(name, list(shape), dt))

    def mksem(name):
        return Sem(es.enter_context(nc.semaphore(name)))

    PE = Eng("pe", mksem("s_pe"), False)
    ACT = Eng("act", mksem("s_act"), True)
    DVE = Eng("dve", mksem("s_dve"), True)
    POOL = Eng("pool", mksem("s_pool"), True)
    SP = Eng("sp", mksem("s_sp"), False)

    wbuf = sb("wbuf", [128, NW, 8, 512], BF16)
    hT = sb("hT", [128, 2, 8, 512], BF16)
    xs = sb("xs", [128, NX, D])
    xr = sb("xr", [128, 2, D])
    xn = sb("xn", [128, 4, D], BF16)
    junk = sb("junk", [128, D], BF16)
    KA = sb("KA", [128, RING, 4, 128], BF16)
    QA = sb("QA", [128, 2, 4, 512], BF16)
    VA = sb("VA", [128, RING, 8, 65], BF16)
    KB = sb("KB", [128, RING, 128], BF16)
    QB = sb("QB", [128, 2, 4, 512], BF16)
    VB = sb("VB", [128, RING, 2, 65], BF16)
    TN = sb("TN", [128, NTN, 1024], BF16)
    TNB = sb("TNB", [128, 4, 1024], BF16)
    TS = sb("TS", [128, 3, 1024], BF16)
    PT = sb("PT", [128, NPT, 512], BF16)
    lerp_tmp = sb("lerp_tmp", [128, 1024], BF16)
    junk2 = lerp_tmp
    sz = sb("sz", [128, 4, 1024], BF16)
    oab = sb("oab", [128, 2, 2, 512], BF16)
    otmp = sb("otmp", [128, 2, 256])
    oT = sb("oT", [128, 8, 512], BF16)
    sg = sb("sg", [128, 2, 512], BF16)
    m12 = sb("m12", [128, 2, 512])
    mT = sb("mT", [128, 8, 512], BF16)
    GG = sb("GG", [128, D])
    t1 = sb("t1", [128, D])
    small = sb("small", [128, 64])
    mod = sb("mod", [128, 2, 8, 8])
    gpb = sb("gpb", [128, 24])
    idb = sb("idb", [128, 128], BF16)
    siluT = sb("siluT", [128, 8, 8])
    adaT = sb("adaT", [128, 16, 8])
    stg = xr
    ggrow = xr[0:8, 0, :]
    ggd = nc.dram_tensor("ggd", [8, D], F32).ap()

    TR = [ps("TR0", [128, 1024], BF16), ps("TR1", [128, 1024], BF16)]
    MM = [ps("MM0", [128, 512]), ps("MM1", [128, 512])]
    ST = [ps("ST0", [128, 512]), ps("ST1", [128, 512])]
    OB = [ps("OB0", [128, 512]), ps("OB1", [128, 512])]

    B = {}
    STR = None

    def bf(name):
        if name not in B:
            B[name] = Buf(name)
        return B[name]

    STR = [(ST[0], bf("ST0")), (ST[1], bf("ST1")), (MM[0], bf("MM0")), (MM[1], bf("MM1"))]
    cnt = {"mm": 0, "st": 0, "e": 0, "pt": 0, "tnb": 0, "x": 0, "xr": 0, "m": 0}
    dma_sems = {}

    def dsem(name):
        if name not in dma_sems:
            dma_sems[name] = mksem("d_" + name)
        return dma_sems[name]

    def dma(eng, out, in_, reads, writes, semname, waw=True):
        return op(eng, I("dma_start", out=out, in_=in_), reads, writes, sem=dsem(semname), inc=16, waw=waw)

    def dump(name, ap, bufs):
        if not debug:
            return
        shp = [int(v) for v in ap.shape]
        dt_ = nc.dram_tensor("dbg_" + name, shp, F32, kind="ExternalOutput").ap()
        dma(POOL, dt_, ap, bufs, [bf("dbgout")], "dbg")

    def ld(out, in_, wbuf_):
        return dma(SP, out, in_, [], [wbuf_], "ld_" + wbuf_.name)

    SM_SSQ, SM_RSTD, SM_MH, SM_ES, SM_F, SM_OMF, SM_Y, SM_RD, SM_FN, SM_OMFN = 0, 4, 8, 9, 17, 25, 33, 40, 48, 56
    smc = bf("sm_c")

    win_r = w_in.rearrange("(kc p) c -> p kc c", p=128)
    conv_gate = []
    conv_evs = []
    cv_events = {}

    def conv(dst, src, uname):
        if len(conv_evs) >= 8:
            POOL.wait(conv_evs[-8])
        ev = dma(POOL, dst, src, list(conv_gate), [bf("cvb_%d" % len(conv_evs))], "cvd_%d" % len(conv_evs), waw=False)
        conv_evs.append(ev)
        cv_events.setdefault(uname, []).append(ev)

    wpa_r = w_pa.rearrange("(kc p) c -> p kc c", p=128)
    wpb_r = w_pb.rearrange("(kc p) c -> p kc c", p=128)
    wout_r = w_out.rearrange("(kc p) c -> p kc c", p=128)

    UI = {nm: i_ for i_, nm in enumerate(U_OFF)}

    def conv_simple(name, c0):
        conv(win_bf[UI[name], :, :, :], win_r[:, :, c0:c0 + 512], name)

    def conv_h(fc):
        u_, nm = UI["H%d" % fc], "H%d" % fc
        conv(win_bf[u_, :, :, 0:128], win_r[:, :, C_GA + fc * 128:C_GA + (fc + 1) * 128], nm)
        conv(win_bf[u_, :, :, 128:256], win_r[:, :, C_GB + fc * 128:C_GB + (fc + 1) * 128], nm)
        conv(win_bf[u_, :, 0:4, 256:384], wpa_r[:, :, fc * 128:(fc + 1) * 128], nm)
        conv(win_bf[u_, :, 4:8, 256:384], wpb_r[:, :, fc * 128:(fc + 1) * 128], nm)

    conv_simple("KA", C_KA)
    conv(win_bf[UI["KVB"], :, :, 0:256], win_r[:, :, C_KB:C_KB + 256], "KVB")
    conv_simple("VA", C_VA)
    conv_simple("QA", C_QA)
    for c in range(4):
        conv(win_bf[UI["QB"], :, :, c * 128:c * 128 + 64], win_r[:, :, C_QB + c * 64:C_QB + c * 64 + 64], "QB")
        conv(win_bf[UI["QB"], :, :, c * 128 + 64:c * 128 + 128],
             win_r[:, :, C_QB + (4 + c) * 64:C_QB + (4 + c) * 64 + 64], "QB")
    ld(small[:, SM_F:SM_F + 8], flg, bf("sm_f"))
    ld(small[:, SM_ES:SM_ES + 8], sink, bf("sm_es"))
    ld(siluT[:], csT, bf("siluT"))
    ld(gpb[:, 0:8], g_preT, bf("gpb0"))
    ld(gpb[:, 8:24], b_adaT, bf("gpb1"))
    ld(xs[0:8, 0, :], b_gate8, bf("xs0"))
    ld(xs[0:8, 1, :], g_post8, bf("xs1"))
    ld(stg[:, 0, 0:128], ident, bf("xr0"))
    op(POOL, I("memset", small[:, SM_MH:SM_MH + 1], -0.5), [], [smc])
    op(DVE, I("tensor_copy", out=idb[:], in_=stg[:, 0, 0:128]), [bf("xr0")], [bf("idb")])
    op(DVE, I("tensor_scalar", out=small[:, SM_OMF:SM_OMF + 8], in0=small[:, SM_F:SM_F + 8], scalar1=-1.0, scalar2=1.0,
              op0=ALU.mult, op1=ALU.add), [bf("sm_f")], [smc])
    op(ACT, I("activation", out=small[:, SM_ES:SM_ES + 8], in_=small[:, SM_ES:SM_ES + 8], func=AF.Exp), [], [bf("sm_es")])
    op(DVE, I("tensor_scalar", out=small[:, SM_FN:SM_FN + 8], in0=small[:, SM_F:SM_F + 8], scalar1=NEG, scalar2=None,
              op0=ALU.mult), [bf("sm_f")], [smc])
    op(DVE, I("tensor_scalar", out=small[:, SM_OMFN:SM_OMFN + 8], in0=small[:, SM_OMF:SM_OMF + 8], scalar1=NEG, scalar2=None,
              op0=ALU.mult), [], [smc])
    op(POOL, I("memset", VA[:, :, :, 64:65], 1.0), [], [bf("VA%d" % s) for s in range(RING)])
    op(POOL, I("memset", VB[:, :, :, 64:65], 1.0), [], [bf("VB%d" % s) for s in range(RING)])

    op(ACT, I("activation", out=siluT[:], in_=siluT[:], func=AF.Silu), [], [bf("siluT")])
    wada_r = w_ada.rearrange("(kc p) c -> p kc c", p=128)
    wst = [(xs[:, 2:4, :], [bf("xs2"), bf("xs3")]), (xr[:, :, :], [bf("xr0"), bf("xr1")])]
    for blk in range(8):
        wt, wtb = wst[blk % 2]
        wv = wt.rearrange("p a (b c) -> p (a b) c", c=256)
        dma(SP, wv, wada_r[:, :, blk * 256:(blk + 1) * 256], [], wtb, "ld_wst%d" % (blk % 2))
        for j2 in range(2):
            fc = blk * 2 + j2
            bank, bb = OB[fc % 2], bf("OB%d" % (fc % 2))
            for kc in range(8):
                op(PE, I("matmul", bank[:, 0:8], lhsT=wv[:, kc, j2 * 128:(j2 + 1) * 128], rhs=siluT[:, kc, :],
                         start=(kc == 0), stop=(kc == 7)), wtb + [bf("siluT")], [bb])
            op(DVE, I("tensor_copy", out=adaT[:, fc, :], in_=bank[:, 0:8]), [], [bb, bf("adaT")])
    op(DVE, I("tensor_tensor", out=adaT[:], in0=adaT[:], in1=gpb[:, 8:24].unsqueeze(2).broadcast_to([128, 16, 8]),
              op=ALU.add), [bf("gpb1")], [bf("adaT")])
    op(DVE, I("tensor_scalar", out=adaT[:, 8:16, :], in0=adaT[:, 8:16, :], scalar1=1.0, scalar2=None, op0=ALU.add),
       [], [bf("adaT")])
    op(DVE, I("tensor_tensor", out=mod[:, 0, :, :], in0=adaT[:, 8:16, :],
              in1=gpb[:, 0:8].unsqueeze(2).broadcast_to([128, 8, 8]), op=ALU.mult), [bf("adaT"), bf("gpb0")], [bf("mod")])
    op(DVE, I("tensor_copy", out=mod[:, 1, :, :], in_=adaT[:, 0:8, :]), [bf("adaT")], [bf("mod")])
    for half in range(2):
        bank, bb = MM[half], bf("MM%d" % half)
        for k2 in range(2):
            wv = xs[:, 2:4, :].rearrange("p a (b c) -> p (a b) c", c=512)
            wtb = [bf("xs2"), bf("xs3")]
            dma(SP, wv, wada_r[:, 4 * k2:4 * k2 + 4, 2048 + half * 512:2048 + (half + 1) * 512], [], wtb, "ld_wst0")
            for kk in range(4):
                kc = 4 * k2 + kk
                op(PE, I("matmul", bank[0:8, :], lhsT=siluT[:, kc, :], rhs=wv[:, kk, :], start=(kc == 0), stop=(kc == 7)),
                   wtb + [bf("siluT")], [bb])
        op(DVE, I("tensor_tensor", out=stg[0:8, 0, half * 512:(half + 1) * 512], in0=bank[0:8, :],
                  in1=xs[0:8, 0, half * 512:(half + 1) * 512], op=ALU.add), [bf("xs0")], [bb, bf("xr0")])
    op(DVE, I("tensor_tensor", out=ggrow, in0=ggrow, in1=xs[0:8, 1, :], op=ALU.mult), [bf("xs1")], [bf("xr0")])
    dma(SP, ggd, ggrow, [bf("xr0")], [bf("ggd")], "ggd")
    def usrc(name):
        return win_bf[UI[name], :, :, 0:U_W[name]], U_W[name]

    def a_units(full):
        return ["QA", "KA", "QB", "KVB", "VA"] if full else ["KA", "KVB", "VA"]

    b_units = ["ZA", "ZB"] + ["H%d" % f_ for f_ in range(8)] + ["OUT0", "OUT1"]
    unit_seq = a_units(False) + a_units(True)
    for k in range(n_tb):
        unit_seq += a_units(k + 1 < n_tb) + b_units
    ws = {"next_load": 0, "next_use": 0, "free": list(range(NW)), "slot": {}}

    def ws_load():
        j = ws["next_load"]
        if j >= len(unit_seq) or not ws["free"]:
            return
        ws["next_load"] += 1
        src, w = usrc(unit_seq[j])
        slot = ws["free"].pop(0)
        ws["slot"][j] = slot
        assert unit_seq[j] in cv_events, unit_seq[j]
        for ev_ in cv_events[unit_seq[j]]:
            SP.wait(ev_)
        dma(SP, wbuf[:, slot, :, 0:w], src, [], [bf("wbuf%d" % slot)], "wb%d" % slot)

    def ws_get(name):
        j = ws["next_use"]
        assert unit_seq[j] == name, (unit_seq[j], name, j)
        assert j < ws["next_load"], (j, name)
        ws["next_use"] += 1
        return ws["slot"][j]

    def ws_release(slot):
        ws["free"].append(slot)
        ws_load()

    for _ in range(NW):
        ws_load()

    def hkc(hb_, kc_):
        return bf("hT%d_%d" % (hb_, kc_))

    def mmbank():
        i = cnt["mm"] % 2
        cnt["mm"] += 1
        return MM[i], bf("MM%d" % i)

    def slot_of(tile):
        return (tile + 4) % RING

    def tb_tiles(k):
        if k == -1:
            return [-2, -1]
        if k == n_tb:
            return [n_own, n_own + 1]
        return [4 * k + t for t in range(4)]

    def tb_slot(k):
        return min(max(k, 0), n_tb - 1) // 4

    xq = {}

    def stage_X(k):
        for t, tile in enumerate(tb_tiles(k)):
            xi = cnt["x"] % NX
            cnt["x"] += 1
            xq[(k, t)] = xi
            dma(ACT, xs[:, xi, :], xe[(tile + 2) * 128:(tile + 3) * 128, :], [], [bf("xs%d" % xi)], "xs%d" % xi)

    def stage_N(k):
        for t, tile in enumerate(tb_tiles(k)):
            xi = xq[(k, t)]
            xb, nb, smn = bf("xs%d" % xi), bf("xn%d" % t), bf("sm_n%d" % t)
            c0, r0 = SM_SSQ + t, SM_RSTD + t
            op(DVE, I("scalar_tensor_tensor", out=junk2[:], in0=xs[:, xi, :], scalar=1.0, in1=xs[:, xi, :], op0=ALU.mult,
                      op1=ALU.mult, accum_out=small[:, c0:c0 + 1]), [xb], [bf("lerp_tmp"), smn])
            op(POOL, I("tensor_scalar", out=small[:, c0:c0 + 1], in0=small[:, c0:c0 + 1], scalar1=1.0 / D, scalar2=1e-6,
                       op0=ALU.mult, op1=ALU.add), [], [smn])
            op(POOL, I("tensor_tensor", out=small[:, r0:r0 + 1], in0=small[:, c0:c0 + 1], in1=small[:, SM_MH:SM_MH + 1],
                       op=ALU.pow), [smc], [smn])
            op(DVE, I("tensor_scalar", out=xn[:, t, :], in0=xs[:, xi, :], scalar1=small[:, r0:r0 + 1], scalar2=None,
                      op0=ALU.mult), [xb, smn], [nb])

    def stage_T(k):
        nt = len(tb_tiles(k))
        hb = k % 2
        s = tb_slot(k)
        hbuf = bf("hT%d" % hb)
        for j in range(4):
            trb, trB = TR[j % 2], bf("TR%d" % (j % 2))
            for kc2 in range(2):
                kc = 2 * j + kc2
                for t in range(nt):
                    op(PE, I("transpose", out=trb[:, kc2 * 512 + t * 128:kc2 * 512 + (t + 1) * 128],
                             in_=xn[:, t, kc * 128:(kc + 1) * 128], identity=idb[:]), [bf("xn%d" % t), bf("idb")], [trB])
            for kc2 in range(2):
                kc = 2 * j + kc2
                op(ACT, I("activation", out=hT[:, hb, kc, 0:nt * 128], in_=trb[:, kc2 * 512:kc2 * 512 + nt * 128],
                          func=AF.Identity, scale=mod[:, 0, kc, s:s + 1], bias=mod[:, 1, kc, s:s + 1]),
                   [bf("mod")], [trB, hkc(hb, kc)])

    def stage_IP(k):
        tiles = tb_tiles(k)
        nt = len(tiles)
        ntok = nt * 128
        full = nt == 4
        hb = k % 2
        hbuf = bf("hT%d" % hb)
        s0 = slot_of(tiles[0])

        def ringb(nm):
            return [bf("%s%d" % (nm, s0 + t)) for t in range(nt)]

        def fm_unit(name, nmc, evac):
            slot = ws_get(name)
            wb = bf("wbuf%d" % slot)
            for mc in range(nmc):
                bank, bb = mmbank()
                for kc in range(8):
                    op(PE, I("matmul", bank[:, 0:ntok], lhsT=wbuf[:, slot, kc, mc * 128:(mc + 1) * 128],
                             rhs=hT[:, hb, kc, 0:ntok], start=(kc == 0), stop=(kc == 7)), [wb, hkc(hb, kc)], [bb])
                evac(mc, bank, bb)
            return slot

        if full:
            ws_release(fm_unit("QA", 4, lambda mc, bank, bb: op(ACT, I("mul", out=QA[:, hb, mc, :], in_=bank[:, :], mul=0.125),
                                                                [], [bb, bf("QA%d" % hb)])))
        ws_release(fm_unit("KA", 4, lambda mc, bank, bb: op(ACT, I("copy", out=KA[:, s0:s0 + nt, mc, :],
                                                                   in_=bank[:, 0:ntok].rearrange("p (a b) -> p a b", b=128)),
                                                            [], [bb] + ringb("KA"))))
        if full:
            ws_release(fm_unit("QB", 4, lambda mc, bank, bb: op(ACT, I("mul", out=QB[:, hb, mc, :], in_=bank[:, :], mul=0.125),
                                                                [], [bb, bf("QB%d" % hb)])))
        slot = fm_unit("KVB", 1, lambda mc, bank, bb: op(ACT, I("copy", out=KB[:, s0:s0 + nt, :],
                                                                in_=bank[:, 0:ntok].rearrange("p (a b) -> p a b", b=128)),
                                                         [], [bb] + ringb("KB")))
        wb = bf("wbuf%d" % slot)
        bank, bb = mmbank()
        for t in range(nt):
            for kc in range(8):
                op(PE, I("matmul", bank[:, t * 128:(t + 1) * 128], lhsT=hT[:, hb, kc, t * 128:(t + 1) * 128],
                         rhs=wbuf[:, slot, kc, 128:256], start=(kc == 0), stop=(kc == 7)), [wb, hkc(hb, kc)], [bb])
        op(DVE, I("tensor_copy", out=VB[:, s0:s0 + nt, :, 0:64],
                  in_=bank[:, 0:ntok].rearrange("p (a b c) -> p a b c", b=2, c=64)), [], [bb] + ringb("VB"))
        ws_release(slot)
        slot = ws_get("VA")
        wb = bf("wbuf%d" % slot)
        for t in range(nt):
            bank, bb = mmbank()
            for kc in range(8):
                op(PE, I("matmul", bank[:, :], lhsT=hT[:, hb, kc, t * 128:(t + 1) * 128], rhs=wbuf[:, slot, kc, :],
                         start=(kc == 0), stop=(kc == 7)), [wb, hkc(hb, kc)], [bb])
            op(DVE, I("tensor_copy", out=VA[:, s0 + t, :, 0:64], in_=bank[:, :].rearrange("p (a b) -> p a b", b=64)),
               [], [bb, bf("VA%d" % (s0 + t))])
        ws_release(slot)

    def tnb_slot():
        i = cnt["tnb"] % 4
        cnt["tnb"] += 1
        return i

    def blend_mul(src_ap, srcbuf, fcol):
        i = tnb_slot()
        ncol = fcol - SM_F + SM_OMFN if fcol < SM_OMF else fcol - SM_OMF + SM_FN
        op(DVE, I("tensor_scalar", out=TNB[:, i, :], in0=src_ap, scalar1=small[:, ncol:ncol + 1], scalar2=None,
                  op0=ALU.add), [srcbuf, smc, bf("sm_f")], [bf("TNB%d" % i)])
        return ("TNB", i)

    def blend_lerp(t_int, t_full, b):
        i = tnb_slot()
        op(DVE, I("tensor_scalar", out=TNB[:, i, :], in0=TN[:, t_full, :], scalar1=small[:, SM_F + b:SM_F + b + 1],
                  scalar2=None, op0=ALU.mult), [bf("TN"), smc, bf("sm_f")], [bf("TNB%d" % i)])
        op(DVE, I("tensor_scalar", out=lerp_tmp[:], in0=TN[:, t_int, :], scalar1=small[:, SM_OMF + b:SM_OMF + b + 1],
                  scalar2=None, op0=ALU.mult), [bf("TN"), smc], [bf("lerp_tmp")])
        op(DVE, I("tensor_tensor", out=TNB[:, i, :], in0=TNB[:, i, :], in1=lerp_tmp[:], op=ALU.add),
           [bf("lerp_tmp")], [bf("TNB%d" % i)])
        return ("TNB", i)

    def na_tables(i):
        bi, b = i % 16, i // 16
        tnb = bf("TN")

        def st(o):
            return ("TN", {-2: T_IM2, -1: T_F[-1], 0: T_F[0], 1: T_F[1], 2: T_IP2}[o])
        if bi == 0:
            f, omf = SM_F + b, SM_OMF + b
            return [(-2, blend_mul(TN[:, T_IM2, :], tnb, omf)), (-1, blend_mul(TN[:, T_F[-1], :], tnb, omf)),
                    (0, st(0)), (1, st(1)), (2, blend_lerp(T_IP2, T_F[2], b)), (3, blend_mul(TN[:, T_F[3], :], tnb, f))]
        if bi == 1:
            omf = SM_OMF + b
            return [(-2, blend_mul(TN[:, T_IM2, :], tnb, omf)), (-1, st(-1)), (0, st(0)), (1, st(1)),
                    (2, blend_lerp(T_IP2, T_F[2], b))]
        if bi == 15:
            f, omf = SM_F + b + 1, SM_OMF + b + 1
            return [(-3, blend_mul(TN[:, T_F[-3], :], tnb, f)), (-2, blend_lerp(T_IM2, T_F[-2], b + 1)),
                    (-1, st(-1)), (0, st(0)), (1, blend_mul(TN[:, T_F[1], :], tnb, omf)),
                    (2, blend_mul(TN[:, T_IP2, :], tnb, omf))]
        if bi == 14:
            omf = SM_OMF + b + 1
            return [(-2, blend_lerp(T_IM2, T_F[-2], b + 1)), (-1, st(-1)), (0, st(0)), (1, st(1)),
                    (2, blend_mul(TN[:, T_IP2, :], tnb, omf))]
        return [(o, st(o)) for o in (-2, -1, 0, 1, 2)]

    def sw_tables(i):
        bi, b = i % 16, i // 16
        res = []
        for o in (-1, 0, 1):
            if o == -1 and bi == 0:
                res.append((o, blend_mul(TS[:, 0, :], bf("TS"), SM_OMF + b)))
            elif o == 1 and bi == 15:
                res.append((o, blend_mul(TS[:, 2, :], bf("TS"), SM_OMF + b + 1)))
            else:
                res.append((o, ("TS", o + 1)))
        return res

    def tbl_ap(ref, half):
        kind, idx = ref
        src = {"TN": TN, "TNB": TNB, "TS": TS}[kind]
        return src[:, idx, half * 512:(half + 1) * 512], bf(kind if kind != "TNB" else "TNB%d" % idx)

    def stage_B(k):
        hb = k % 2
        hbuf = bf("hT%d" % hb)
        s = k // 4
        if k == 0:
            late_convs(1)
        if k % 4 == 0:
            dma(SP, GG[:], ggd[s:s + 1, :].broadcast_to([128, D]), [bf("ggd")], [bf("GG")], "ld_GG")
        for zi, name in enumerate(("ZA", "ZB")):
            slot = ws_get(name)
            wb = bf("wbuf%d" % slot)
            for t in range(4):
                bank, bb = mmbank()
                for kc in range(8):
                    op(PE, I("matmul", bank[:, :], lhsT=hT[:, hb, kc, t * 128:(t + 1) * 128], rhs=wbuf[:, slot, kc, :],
                             start=(kc == 0), stop=(kc == 7)), [wb, hkc(hb, kc)], [bb])
                op(ACT, I("activation", out=sz[:, t, zi * 512:(zi + 1) * 512], in_=bank[:, :], func=AF.Silu),
                   [], [bb, bf("sz%d" % t)])
            ws_release(slot)

        for t in range(4):
            i = 4 * k + t
            if k == 0 and t >= 1:
                late_convs(t + 1)

            def emit_st_pair(kind, o, ref):
                ks = slot_of(i + o)
                banks = []
                for hf in range(2):
                    sti = cnt["st"] % 4
                    cnt["st"] += 1
                    banks.append(STR[sti])
                for hf in range(2):
                    bank, bb = banks[hf]
                    tap, tbuf = tbl_ap(ref, hf)
                    op(PE, I("matmul", bank[:, :], lhsT=idb[:], rhs=tap, start=True, stop=False, skip_group_check=True),
                       [bf("idb"), tbuf], [bb])
                if kind == "na":
                    for hh in range(4):
                        for hf in range(2):
                            bank, bb = banks[hf]
                            op(PE, I("matmul", bank[:, hh * 128:(hh + 1) * 128], lhsT=KA[hf * 64:(hf + 1) * 64, ks, hh, :],
                                     rhs=QA[hf * 64:(hf + 1) * 64, hb, hh, t * 128:(t + 1) * 128], start=False, stop=(hh == 3),
                                     skip_group_check=True), [bf("KA%d" % ks), bf("QA%d" % hb)], [bb])
                else:
                    for hf in range(2):
                        bank, bb = banks[hf]
                        op(PE, I("matmul", bank[:, :], lhsT=KB[hf * 64:(hf + 1) * 64, ks, :],
                                 rhs=QB[hf * 64:(hf + 1) * 64, hb, :, t * 128:(t + 1) * 128], start=False, stop=True,
                                 skip_group_check=True), [bf("KB%d" % ks), bf("QB%d" % hb)], [bb])
                pis = []
                for hf in range(2):
                    bank, bb = banks[hf]
                    pi = cnt["pt"] % NPT
                    cnt["pt"] += 1
                    op(ACT, I("activation", out=PT[:, pi, :], in_=bank[:, :], func=AF.Exp), [], [bb, bf("PT%d" % pi)])
                    pis.append(pi)
                return pis

            def emit_pv(step, pi, first, last):
                kind, o, hf, ref = step
                ks = slot_of(i + o)
                obb = bf("OB%d" % hf)
                ov = OB[hf][:, 0:260].rearrange("p (h e) -> p h e", e=65)
                for hh in range(4):
                    if kind == "na":
                        rhs, rb = VA[:, ks, 2 * hh + hf, :], bf("VA%d" % ks)
                    else:
                        rhs, rb = VB[:, ks, hf, :], bf("VB%d" % ks)
                    op(PE, I("matmul", ov[:, hh, :], lhsT=PT[:, pi, hh * 128:(hh + 1) * 128], rhs=rhs,
                             start=(first and hh == 0), stop=(last and hh == 3), skip_group_check=True),
                       [bf("PT%d" % pi), rb], [obb])

            def finalize(kind):
                oi = 0 if kind == "na" else 1
                for hf in range(2):
                    obb, smr = bf("OB%d" % hf), bf("sm_rd%d" % hf)
                    ov = OB[hf][:, 0:260].rearrange("p (h e) -> p h e", e=65)
                    rd = small[:, SM_RD + hf * 4:SM_RD + hf * 4 + 4]
                    if kind == "na":
                        op(DVE, I("reciprocal", out=rd, in_=ov[:, :, 64]), [], [obb, smr])
                    else:
                        op(DVE, I("tensor_tensor", out=rd, in0=ov[:, :, 64], in1=small[:, SM_ES + 4 * hf:SM_ES + 4 * hf + 4],
                                  op=ALU.add), [bf("sm_es")], [obb, smr])
                        op(DVE, I("reciprocal", out=rd, in_=rd), [], [smr])
                    tv = otmp[:, hf, :].rearrange("p (h d) -> p h d", d=64)
                    op(DVE, I("tensor_tensor", out=tv, in0=ov[:, :, 0:64], in1=rd.unsqueeze(2).broadcast_to([128, 4, 64]),
                              op=ALU.mult), [smr], [obb, bf("otmp%d" % hf)])
                    if kind == "na":
                        dst = oab[:, t % 2, 0, :].rearrange("p (hh two d) -> p hh two d", two=2, d=64)[:, :, hf, :]
                        zz = sz[:, t, 0:512].rearrange("p (hh two d) -> p hh two d", two=2, d=64)[:, :, hf, :]
                    else:
                        dst = oab[:, t % 2, 1, hf * 256:(hf + 1) * 256].rearrange("p (h d) -> p h d", d=64)
                        zz = sz[:, t, 512 + hf * 256:512 + (hf + 1) * 256].rearrange("p (h d) -> p h d", d=64)
                    op(POOL, I("tensor_tensor", out=dst, in0=tv, in1=zz, op=ALU.mult),
                       [bf("otmp%d" % hf), bf("sz%d" % t)], [bf("oab%d_%d" % (t % 2, oi))])

            def run_steps(kind, tables):
                n = 2 * len(tables)
                pend = []
                started = set()

                def flush_one():
                    pj, pst, ppi = pend.pop(0)
                    emit_pv(pst, ppi, pst[2] not in started, pj >= n - 2)
                    started.add(pst[2])
                for jo, (o, ref) in enumerate(tables):
                    pis = emit_st_pair(kind, o, ref)
                    for hf in range(2):
                        pend.append((2 * jo + hf, (kind, o, hf, ref), pis[hf]))
                    while len(pend) > 2:
                        flush_one()
                while pend:
                    flush_one()
                finalize(kind)

            def emit_tr(tt):
                trb, trB = TR[tt % 2], bf("TR%d" % (tt % 2))
                for j in range(8):
                    op(PE, I("transpose", out=trb[:, j * 128:(j + 1) * 128],
                             in_=oab[:, tt % 2, j // 4, (j % 4) * 128:(j % 4 + 1) * 128], identity=idb[:]),
                       [bf("oab%d_%d" % (tt % 2, j // 4)), bf("idb")], [trB])
                op(DVE, I("tensor_copy", out=oT[:, :, tt * 128:(tt + 1) * 128],
                          in_=trb[:, :].rearrange("p (a b) -> p a b", b=128)), [], [trB, bf("oT")])

            run_steps("na", na_tables(i))
            if t > 0:
                emit_tr(t - 1)
            run_steps("sw", sw_tables(i))
            if k == 0:
                dump("oab%d" % t, oab[:, t % 2], [bf("oab%d_0" % (t % 2)), bf("oab%d_1" % (t % 2))])
                dump("sz%d" % t, sz[:, t], [bf("sz%d" % t)])

        for fc in range(8):
            sh = ws_get("H%d" % fc)
            whb = bf("wbuf%d" % sh)
            for gi in range(2):
                bank, bb = ST[gi], bf("ST%d" % gi)
                for kc in range(8):
                    op(PE, I("matmul", bank[:, :], lhsT=wbuf[:, sh, kc, gi * 128:(gi + 1) * 128], rhs=hT[:, hb, kc, :],
                             start=(kc == 0), stop=(kc == 7)), [whb, hkc(hb, kc)], [bb])
                op(ACT, I("activation", out=sg[:, gi, :], in_=bank[:, :], func=AF.Sigmoid), [], [bb, bf("sg%d" % gi)])
            if fc == 0:
                emit_tr(3)
            for pi_ in range(2):
                bank, bb = MM[pi_], bf("MM%d" % pi_)
                for c in range(4):
                    op(PE, I("matmul", bank[:, :], lhsT=wbuf[:, sh, pi_ * 4 + c, 256:384], rhs=oT[:, pi_ * 4 + c, :],
                             start=(c == 0), stop=(c == 3)), [whb, bf("oT")], [bb])
                op(DVE, I("tensor_tensor", out=m12[:, pi_, :], in0=bank[:, :], in1=sg[:, pi_, :], op=ALU.mult),
                   [bf("sg%d" % pi_)], [bb, bf("m12_%d" % pi_)])
            op(POOL, I("tensor_tensor", out=mT[:, fc, :], in0=m12[:, 0, :], in1=m12[:, 1, :], op=ALU.add),
               [bf("m12_0"), bf("m12_1")], [bf("mT")])
            ws_release(sh)

        if k == 0:
            dump("oT", oT[:], [bf("oT")])
            dump("mT", mT[:], [bf("mT")])
            dump("GG", GG[:], [bf("GG")])
        so = [ws_get("OUT0"), ws_get("OUT1")]
        ybanks = [[(ST[0], bf("ST0")), (ST[1], bf("ST1"))], [(OB[0], bf("OB0")), (OB[1], bf("OB1"))],
                  [(MM[0], bf("MM0")), (MM[1], bf("MM1"))]]
        for t in range(4):
            i = 4 * k + t
            xi = cnt["xr"] % 2
            cnt["xr"] += 1
            smy = bf("sm_y%d" % (t % 2))
            SY = SM_Y + 4 * (t % 2)
            xrb = bf("xr%d" % xi)
            dma(POOL, xr[:, xi, :], xe[(i + 2) * 128:(i + 3) * 128, :], [], [xrb], "xr%d" % xi)
            for half in range(2):
                bank, bb = ybanks[t % 2][half]
                for kc in range(8):
                    op(PE, I("matmul", bank[:, :], lhsT=mT[:, kc, t * 128:(t + 1) * 128], rhs=wbuf[:, so[half], kc, :],
                             start=(kc == 0), stop=(kc == 7)), [bf("wbuf%d" % so[half]), bf("mT")], [bb])
                op(ACT, I("activation", out=junk[:, 0:512], in_=bank[:, :], func=AF.Square,
                          accum_out=small[:, SY + half:SY + half + 1]), [], [bb, bf("junk"), smy])
            op(POOL, I("tensor_tensor", out=small[:, SY + 2:SY + 3], in0=small[:, SY:SY + 1],
                       in1=small[:, SY + 1:SY + 2], op=ALU.add), [], [smy])
            op(POOL, I("tensor_scalar", out=small[:, SY + 2:SY + 3], in0=small[:, SY + 2:SY + 3], scalar1=1.0 / D,
                       scalar2=1e-6, op0=ALU.mult, op1=ALU.add), [], [smy])
            op(POOL, I("tensor_tensor", out=small[:, SY + 3:SY + 4], in0=small[:, SY + 2:SY + 3],
                       in1=small[:, SM_MH:SM_MH + 1], op=ALU.pow), [smc], [smy])
            for half in range(2):
                bank, bb = ybanks[t % 2][half]
                op(DVE, I("scalar_tensor_tensor", out=t1[:, half * 512:(half + 1) * 512], in0=bank[:, :],
                          scalar=small[:, SY + 3:SY + 4], in1=GG[:, half * 512:(half + 1) * 512], op0=ALU.mult,
                          op1=ALU.mult), [smy, bf("GG")], [bb, bf("t1")])
            op(POOL, I("tensor_tensor", out=xr[:, xi, :], in0=xr[:, xi, :], in1=t1[:], op=ALU.add), [bf("t1")], [xrb])
            dma(POOL, y[i * 128:(i + 1) * 128, :], xr[:, xi, :], [xrb], [bf("yout")], "yst%d" % xi)
        ws_release(so[0])
        ws_release(so[1])

    def setup_tables():
        ld(t1[:], cmask, bf("t1"))
        for o in range(-3, 4):
            sl = (o + 3) % 2
            sbuf_ = bf("xr%d" % sl)
            ld(stg[:, sl, :], rpbx[:, o + 3, :], sbuf_)
            op(DVE, I("tensor_tensor", out=TN[:, T_F[o], :], in0=stg[:, sl, :], in1=t1[:], op=ALU.add),
               [sbuf_, bf("t1")], [bf("TN")])
        for j, (o, ti) in enumerate(((-2, T_IM2), (2, T_IP2))):
            sbuf_ = bf("xr%d" % j)
            ld(stg[:, j, :], rmask[:, j, :], sbuf_)
            op(DVE, I("tensor_tensor", out=TN[:, ti, :], in0=TN[:, T_F[o], :], in1=stg[:, j, :], op=ALU.add),
               [sbuf_], [bf("TN")])
        ld(stg[:, 0, 0:384], swd.rearrange("p a b -> p (a b)"), bf("xr0"))
        xr0v = t1[:].rearrange("p (a b) -> p a b", b=128)
        for o in range(3):
            for h in range(8):
                slope = float(2.0 ** (-(h + 1)))
                op(ACT, I("activation", out=xr0v[:, h, :], in_=stg[:, 0, o * 128:(o + 1) * 128], func=AF.Copy, scale=-slope),
                   [bf("xr0")], [bf("t1")])
            ld(stg[:, 1, :], swm[:, o, :], bf("xr1"))
            op(DVE, I("tensor_tensor", out=TS[:, o, :], in0=t1[:], in1=stg[:, 1, :], op=ALU.add),
               [bf("t1"), bf("xr1")], [bf("TS")])


    def late_convs(stage):
        if stage == 0:
            conv_simple("ZA", C_ZA)
            conv_simple("ZB", C_ZB)
            conv_h(0)
            conv_h(1)
        elif stage == 1:
            conv_h(2)
            conv_h(3)
        elif stage in (2, 3):
            conv_h(2 * stage)
            conv_h(2 * stage + 1)
        elif stage == 4:
            for half in range(2):
                conv(win_bf[UI["OUT%d" % half], :, :, :], wout_r[:, :, half * 512:(half + 1) * 512], "OUT%d" % half)

    stage_X(-1)
    stage_N(-1)
    stage_X(0)
    stage_T(-1)
    stage_IP(-1)
    stage_N(0)
    late_convs(0)
    stage_X(1 if n_tb > 1 else n_tb)
    stage_T(0)
    dump("hT", hT[:, 0], [hkc(0, kc_) for kc_ in range(8)])
    dump("mod", mod[:], [bf("mod")])
    stage_IP(0)
    dump("QA", QA[:, 0], [bf("QA0")])
    dump("KA", KA[:, 4:8], [bf("KA%d" % s_) for s_ in range(4, 8)])
    dump("VA", VA[:, 4:8], [bf("VA%d" % s_) for s_ in range(4, 8)])
    dump("QB", QB[:, 0], [bf("QB0")])
    dump("KB", KB[:, 4:8], [bf("KB%d" % s_) for s_ in range(4, 8)])
    dump("VB", VB[:, 4:8], [bf("VB%d" % s_) for s_ in range(4, 8)])
    stage_N(1 if n_tb > 1 else n_tb)
    setup_tables()
    for k in range(n_tb):
        stage_T(k + 1)
        if k + 2 <= n_tb:
            stage_X(k + 2)
        stage_IP(k + 1)
        if k + 2 <= n_tb:
            stage_N(k + 2)
        stage_B(k)
    for nm in ("yst0", "yst1"):
        s_ = dsem(nm)
        if s_.v:
            POOL.wait((s_, s_.v))

    with nc.Block() as block:
        def emit(engobj):
            def body(e):
                for item in engobj.prog:
                    if item[0] == "w":
                        e.wait_ge(item[1], item[2])
                    else:
                        item[1](e).then_inc(item[2], item[3])
            return body
        block.tensor(emit(PE))
        block.scalar(emit(ACT))
        block.vector(emit(DVE))
        block.gpsimd(emit(POOL))
        block.sync(emit(SP))
    es.close()
    return nc


def _const_tables():
    GW, KW = 64, 16
    col = np.arange(GW)
    cstart = np.clip(col - KW // 2, 0, GW - KW)
    ki = np.arange(128)
    kr, kc = ki // 64, ki % 64
    qr, qc = ki // 64, ki % 64
    cval = ((kc[:, None] >= cstart[qc][None, :]) & (kc[:, None] < cstart[qc][None, :] + KW)).astype(np.float32)
    dc = np.clip(kc[:, None] - qc[None, :] + KW - 1, 0, 2 * KW - 2)
    dr_rel = kr[:, None] - qr[None, :]
    dr_idx = {}
    rmask = {}
    for o in range(-3, 4):
        dr = 2 * o + dr_rel
        dr_idx[o] = np.clip(dr + 7, 0, 14)
        rmask[o] = ((dr >= -4) & (dr <= 3)).astype(np.float32)
    k = np.arange(128)
    swd = np.stack([np.abs(128 * o + k[:, None] - k[None, :]) for o in (-1, 0, 1)], 1).astype(np.float32)
    swm = (swd <= 128).astype(np.float32)
    return cval, dc, dr_idx, rmask, swd, swm


def prep_inputs(X, c_seq, seq_len, n_tb, ncores, weights):
    (w_ada, b_ada, g_pre, g_post, w_in, na_rpb, sw_sink, w_pa, w_pb, w_out) = weights
    tok_per_core = n_tb * 512
    ntot = X.shape[0]
    assert tok_per_core * ncores == ntot
    cval, dc, dr_idx, rmask, swd, swm = _const_tables()
    rpbx = np.zeros((128, 7, 8, 128), np.float32)
    for o in range(-3, 4):
        for hi, h in enumerate(HEAD_ORDER):
            rpbx[:, o + 3, hi, :] = na_rpb[h][dr_idx[o], dc]
    rpbx = rpbx.reshape(128, 7, 1024)
    NEGH = 30000.0
    cmask = np.ascontiguousarray(np.broadcast_to((cval[:, None, :] - 1.0) * NEGH, (128, 8, 128))).reshape(128, 1024).astype(np.float32)
    rm = np.stack([np.broadcast_to((rmask[-2][:, None, :] - 1.0) * NEGH, (128, 8, 128)).reshape(128, 1024),
                   np.broadcast_to((rmask[2][:, None, :] - 1.0) * NEGH, (128, 8, 128)).reshape(128, 1024)], 1).astype(np.float32)
    swm8 = np.ascontiguousarray(np.broadcast_to((swm[:, :, None, :] - 1.0) * NEGH, (128, 3, 8, 128))).reshape(128, 3, 1024).astype(np.float32)
    sel = np.zeros((8, 8, 128), np.float32)
    for s in range(8):
        sel[s, s, :] = 1.0
    sel = sel.reshape(8, 1024)
    common = {
        "w_ada": np.ascontiguousarray(w_ada), "w_in": np.ascontiguousarray(w_in),
        "w_pa": np.ascontiguousarray(w_pa), "w_pb": np.ascontiguousarray(w_pb), "w_out": np.ascontiguousarray(w_out),
        "b_adaT": np.ascontiguousarray(b_ada[:2048].reshape(16, 128).T),
        "b_gate8": np.ascontiguousarray(np.broadcast_to(b_ada[2048:][None, :], (8, D))),
        "g_preT": np.ascontiguousarray(g_pre.reshape(8, 128).T),
        "g_post8": np.ascontiguousarray(np.broadcast_to(g_post[None, :], (8, D))),
        "sink": np.ascontiguousarray(np.broadcast_to(sw_sink[None, :], (128, 8))),
        "rpbx": rpbx, "cmask": cmask, "rmask": np.ascontiguousarray(rm),
        "swd": np.ascontiguousarray(swd), "swm": swm8, "ident": np.eye(128, dtype=np.float32),
    }
    in_maps = []
    for c in range(ncores):
        start = c * tok_per_core
        xe = np.zeros((tok_per_core + 512, D), np.float32)
        lo, hi = start - 256, start + tok_per_core + 256
        slo, shi = max(lo, 0), min(hi, ntot)
        xe[slo - lo:shi - lo] = X[slo:shi]
        csT = np.zeros((128, 8, 8), np.float32)
        flg = np.zeros((128, 8), np.float32)
        for s in range(n_tb // 4):
            seq = (start + 2048 * s) // seq_len
            csT[:, :, s] = c_seq[seq].reshape(8, 128).T
        for b in range(n_tb // 4 + 1):
            flg[:, b] = 1.0 if (start + 2048 * b) % seq_len == 0 else 0.0
        m = dict(common)
        m["xe"] = xe
        m["csT"] = csT
        m["flg"] = flg
        in_maps.append(m)
    return in_maps


_NC_CACHE = {}
_LAST = None


def run(X, c_seq, seq_len, n_tb, ncores, weights):
    in_maps = prep_inputs(X, c_seq, seq_len, n_tb, ncores, weights)
    if n_tb not in _NC_CACHE:
        _NC_CACHE[n_tb] = build(n_tb)
    res = run_bass_kernel_spmd(_NC_CACHE[n_tb], in_maps, core_ids=list(range(ncores)))
    global _LAST
    _LAST = res.results
    return np.concatenate([r["y"] for r in res.results], axis=0)


def kernel(x_prompt, x_sample, c_prompt, c_sample, w_ada, b_ada, g_pre, g_post, w_in, na_rpb, sw_sink,
           w_pa, w_pb, w_out):
    f = lambda a: np.asarray(a, dtype=np.float32)
    xp, xs_ = f(x_prompt), f(x_sample)
    X = np.concatenate([xp.reshape(-1, D), xs_.reshape(-1, D)], axis=0)
    c_seq = np.concatenate([f(c_prompt), f(c_sample)], axis=0)
    weights = (f(w_ada)[0], f(b_ada)[0], f(g_pre)[0], f(g_post)[0], f(w_in)[0], f(na_rpb)[0], f(sw_sink)[0],
               f(w_pa)[0], f(w_pb)[0], f(w_out)[0])
    Y = run(X, c_seq, 8192, 20, NCORES, weights)
    n_p = xp.shape[0] * xp.shape[1]
    return (Y[:n_p].reshape(xp.shape).astype(np.float32), Y[n_p:].reshape(xs_.shape).astype(np.float32))
```

```python
import numpy as np
from contextlib import ExitStack
import concourse.bass as bass
import concourse.mybir as mybir
from concourse.bass_utils import run_bass_kernel_spmd

F32 = mybir.dt.float32
BF16 = mybir.dt.bfloat16
AF = mybir.ActivationFunctionType
ALU = mybir.AluOpType

D = 1024
D_IN = 5376
NCORES = 8
RING = 12
C_QA, C_KA, C_VA, C_ZA, C_QB, C_KB, C_VB, C_ZB, C_GA, C_GB = 0, 512, 1024, 1536, 2048, 2560, 2688, 2816, 3328, 4352
U_OFF = {"QA": 0, "KA": 512, "QB": 1024, "KVB": 1536, "VA": 1792, "ZA": 2304, "ZB": 2816,
         "H0": 0, "H1": 0, "H2": 0, "H3": 0, "H4": 0, "H5": 0, "H6": 0, "H7": 0, "OUT0": 0, "OUT1": 0}
NW = 3
LOOK = 3
NE = 4
NEG = -30000.0
NX = 4
NPT = 5
U_W = {k: 512 for k in U_OFF}
U_W["KVB"] = 256
for _h in range(8):
    U_W["H%d" % _h] = 384
HEAD_ORDER = [0, 2, 4, 6, 1, 3, 5, 7]
T_F = {-3: 0, -2: 1, -1: 2, 0: 3, 1: 4, 2: 5, 3: 6}
T_IM2, T_IP2 = 7, 8
NTN = 9


class Sem:
    def __init__(self, h):
        self.h = h
        self.v = 0


class Eng:
    def __init__(self, name, sem, self_sync):
        self.name = name
        self.sem = sem
        self.seen = {}
        self.prog = []
        self.self_sync = self_sync

    def wait(self, ev):
        s, v = ev
        if s is self.sem and not self.self_sync:
            return
        if self.seen.get(id(s), 0) >= v:
            return
        self.seen[id(s)] = v
        self.prog.append(("w", s.h, v))


class Buf:
    def __init__(self, name=""):
        self.name = name
        self.w = None
        self.r = {}


def op(eng, fn, reads=(), writes=(), sem=None, inc=1, waw=True):
    for b in reads:
        if b.w is not None:
            eng.wait(b.w)
    for b in writes:
        if b.w is not None and waw:
            eng.wait(b.w)
        for s, v in b.r.values():
            eng.wait((s, v))
    s = sem if sem is not None else eng.sem
    s.v += inc
    ev = (s, s.v)
    eng.prog.append(("i", fn, s.h, inc))
    for b in reads:
        b.r[id(s)] = ev
    for b in writes:
        b.w = ev
        b.r = {}
    return ev


def I(meth, *a, **kw):
    return lambda e: getattr(e, meth)(*a, **kw)


def build(n_tb, debug=False):
    assert n_tb % 4 == 0
    n_own = n_tb * 4
    nc = bass.Bass("TRN2", target_bir_lowering=False)
    es = ExitStack()

    def din(name, shape, dt=F32):
        return nc.dram_tensor(name, list(shape), dt, kind="ExternalInput").ap()

    xe = din("xe", [(n_own + 4) * 128, D])
    csT = din("csT", [128, 8, 8])
    flg = din("flg", [128, 8])
    w_ada = din("w_ada", [D, 3 * D])
    b_adaT = din("b_adaT", [128, 16])
    b_gate8 = din("b_gate8", [8, D])
    g_preT = din("g_preT", [128, 8])
    g_post8 = din("g_post8", [8, D])
    w_in = din("w_in", [D, D_IN])
    w_pa = din("w_pa", [512, D])
    w_pb = din("w_pb", [512, D])
    w_out = din("w_out", [D, D])
    sink = din("sink", [128, 8])
    rpbx = din("rpbx", [128, 7, 1024])
    cmask = din("cmask", [128, 1024])
    rmask = din("rmask", [128, 2, 1024])
    swd = din("swd", [128, 3, 128])
    swm = din("swm", [128, 3, 1024])
    ident = din("ident", [128, 128])
    y = nc.dram_tensor("y", [n_own * 128, D], F32, kind="ExternalOutput").ap()
    win_bf = nc.dram_tensor("win_bf", [len(U_OFF), 128, 8, 512], BF16).ap()

    def sb(name, shape, dt=F32):
        return es.enter_context(nc.sbuf_tensor(name, list(shape), dt))

    def ps(name, shape, dt=F32):
        return es.enter_context(nc.psum_tensor(name, list(shape), dt))

    def mksem(name):
        return Sem(es.enter_context(nc.semaphore(name)))

    PE = Eng("pe", mksem("s_pe"), False)
    ACT = Eng("act", mksem("s_act"), True)
    DVE = Eng("dve", mksem("s_dve"), True)
    POOL = Eng("pool", mksem("s_pool"), True)
    SP = Eng("sp", mksem("s_sp"), False)

    wbuf = sb("wbuf", [128, NW, 8, 512], BF16)
    hT = sb("hT", [128, 2, 8, 512], BF16)
    xs = sb("xs", [128, NX, D])
    xr = sb("xr", [128, 2, D])
    xn = sb("xn", [128, 4, D], BF16)
    junk = sb("junk", [128, D], BF16)
    KA = sb("KA", [128, RING, 4, 128], BF16)
    QA = sb("QA", [128, 2, 4, 512], BF16)
    VA = sb("VA", [128, RING, 8, 65], BF16)
    KB = sb("KB", [128, RING, 128], BF16)
    QB = sb("QB", [128, 2, 4, 512], BF16)
    VB = sb("VB", [128, RING, 2, 65], BF16)
    TN = sb("TN", [128, NTN, 1024], BF16)
    TNB = sb("TNB", [128, 4, 1024], BF16)
    TS = sb("TS", [128, 3, 1024], BF16)
    PT = sb("PT", [128, NPT, 512], BF16)
    lerp_tmp = sb("lerp_tmp", [128, 1024], BF16)
    junk2 = lerp_tmp
    sz = sb("sz", [128, 4, 1024], BF16)
    oab = sb("oab", [128, 2, 2, 512], BF16)
    otmp = sb("otmp", [128, 2, 256])
    oT = sb("oT", [128, 8, 512], BF16)
    sg = sb("sg", [128, 2, 512], BF16)
    m12 = sb("m12", [128, 2, 512])
    mT = sb("mT", [128, 8, 512], BF16)
    GG = sb("GG", [128, D])
    t1 = sb("t1", [128, D])
    small = sb("small", [128, 64])
    mod = sb("mod", [128, 2, 8, 8])
    gpb = sb("gpb", [128, 24])
    idb = sb("idb", [128, 128], BF16)
    siluT = sb("siluT", [128, 8, 8])
    adaT = sb("adaT", [128, 16, 8])
    stg = xr
    ggrow = xr[0:8, 0, :]
    ggd = nc.dram_tensor("ggd", [8, D], F32).ap()

    TR = [ps("TR0", [128, 1024], BF16), ps("TR1", [128, 1024], BF16)]
    MM = [ps("MM0", [128, 512]), ps("MM1", [128, 512])]
    ST = [ps("ST0", [128, 512]), ps("ST1", [128, 512])]
    OB = [ps("OB0", [128, 512]), ps("OB1", [128, 512])]

    B = {}
    STR = None

    def bf(name):
        if name not in B:
            B[name] = Buf(name)
        return B[name]

    STR = [(ST[0], bf("ST0")), (ST[1], bf("ST1")), (MM[0], bf("MM0")), (MM[1], bf("MM1"))]
    cnt = {"mm": 0, "st": 0, "e": 0, "pt": 0, "tnb": 0, "x": 0, "xr": 0, "m": 0}
    dma_sems = {}

    def dsem(name):
        if name not in dma_sems:
            dma_sems[name] = mksem("d_" + name)
        return dma_sems[name]

    def dma(eng, out, in_, reads, writes, semname, waw=True):
        return op(eng, I("dma_start", out=out, in_=in_), reads, writes, sem=dsem(semname), inc=16, waw=waw)

    def dump(name, ap, bufs):
        if not debug:
            return
        shp = [int(v) for v in ap.shape]
        dt_ = nc.dram_tensor("dbg_" + name, shp, F32, kind="ExternalOutput").ap()
        dma(POOL, dt_, ap, bufs, [bf("dbgout")], "dbg")

    def ld(out, in_, wbuf_):
        return dma(SP, out, in_, [], [wbuf_], "ld_" + wbuf_.name)

    SM_SSQ, SM_RSTD, SM_MH, SM_ES, SM_F, SM_OMF, SM_Y, SM_RD, SM_FN, SM_OMFN = 0, 4, 8, 9, 17, 25, 33, 40, 48, 56
    smc = bf("sm_c")

    win_r = w_in.rearrange("(kc p) c -> p kc c", p=128)
    conv_gate = []
    conv_evs = []
    cv_events = {}

    def conv(dst, src, uname):
        if len(conv_evs) >= 8:
            POOL.wait(conv_evs[-8])
        ev = dma(POOL, dst, src, list(conv_gate), [bf("cvb_%d" % len(conv_evs))], "cvd_%d" % len(conv_evs), waw=False)
        conv_evs.append(ev)
        cv_events.setdefault(uname, []).append(ev)

    wpa_r = w_pa.rearrange("(kc p) c -> p kc c", p=128)
    wpb_r = w_pb.rearrange("(kc p) c -> p kc c", p=128)
    wout_r = w_out.rearrange("(kc p) c -> p kc c", p=128)

    UI = {nm: i_ for i_, nm in enumerate(U_OFF)}

    def conv_simple(name, c0):
        conv(win_bf[UI[name], :, :, :], win_r[:, :, c0:c0 + 512], name)

    def conv_h(fc):
        u_, nm = UI["H%d" % fc], "H%d" % fc
        conv(win_bf[u_, :, :, 0:128], win_r[:, :, C_GA + fc * 128:C_GA + (fc + 1) * 128], nm)
        conv(win_bf[u_, :, :, 128:256], win_r[:, :, C_GB + fc * 128:C_GB + (fc + 1) * 128], nm)
        conv(win_bf[u_, :, 0:4, 256:384], wpa_r[:, :, fc * 128:(fc + 1) * 128], nm)
        conv(win_bf[u_, :, 4:8, 256:384], wpb_r[:, :, fc * 128:(fc + 1) * 128], nm)

    conv_simple("KA", C_KA)
    conv(win_bf[UI["KVB"], :, :, 0:256], win_r[:, :, C_KB:C_KB + 256], "KVB")
    conv_simple("VA", C_VA)
    conv_simple("QA", C_QA)
    for c in range(4):
        conv(win_bf[UI["QB"], :, :, c * 128:c * 128 + 64], win_r[:, :, C_QB + c * 64:C_QB + c * 64 + 64], "QB")
        conv(win_bf[UI["QB"], :, :, c * 128 + 64:c * 128 + 128],
             win_r[:, :, C_QB + (4 + c) * 64:C_QB + (4 + c) * 64 + 64], "QB")
    ld(small[:, SM_F:SM_F + 8], flg, bf("sm_f"))
    ld(small[:, SM_ES:SM_ES + 8], sink, bf("sm_es"))
    ld(siluT[:], csT, bf("siluT"))
    ld(gpb[:, 0:8], g_preT, bf("gpb0"))
    ld(gpb[:, 8:24], b_adaT, bf("gpb1"))
    ld(xs[0:8, 0, :], b_gate8, bf("xs0"))
    ld(xs[0:8, 1, :], g_post8, bf("xs1"))
    ld(stg[:, 0, 0:128], ident, bf("xr0"))
    op(POOL, I("memset", small[:, SM_MH:SM_MH + 1], -0.5), [], [smc])
    op(DVE, I("tensor_copy", out=idb[:], in_=stg[:, 0, 0:128]), [bf("xr0")], [bf("idb")])
    op(DVE, I("tensor_scalar", out=small[:, SM_OMF:SM_OMF + 8], in0=small[:, SM_F:SM_F + 8], scalar1=-1.0, scalar2=1.0,
              op0=ALU.mult, op1=ALU.add), [bf("sm_f")], [smc])
    op(ACT, I("activation", out=small[:, SM_ES:SM_ES + 8], in_=small[:, SM_ES:SM_ES + 8], func=AF.Exp), [], [bf("sm_es")])
    op(DVE, I("tensor_scalar", out=small[:, SM_FN:SM_FN + 8], in0=small[:, SM_F:SM_F + 8], scalar1=NEG, scalar2=None,
              op0=ALU.mult), [bf("sm_f")], [smc])
    op(DVE, I("tensor_scalar", out=small[:, SM_OMFN:SM_OMFN + 8], in0=small[:, SM_OMF:SM_OMF + 8], scalar1=NEG, scalar2=None,
              op0=ALU.mult), [], [smc])
    op(POOL, I("memset", VA[:, :, :, 64:65], 1.0), [], [bf("VA%d" % s) for s in range(RING)])
    op(POOL, I("memset", VB[:, :, :, 64:65], 1.0), [], [bf("VB%d" % s) for s in range(RING)])

    op(ACT, I("activation", out=siluT[:], in_=siluT[:], func=AF.Silu), [], [bf("siluT")])
    wada_r = w_ada.rearrange("(kc p) c -> p kc c", p=128)
    wst = [(xs[:, 2:4, :], [bf("xs2"), bf("xs3")]), (xr[:, :, :], [bf("xr0"), bf("xr1")])]
    for blk in range(8):
        wt, wtb = wst[blk % 2]
        wv = wt.rearrange("p a (b c) -> p (a b) c", c=256)
        dma(SP, wv, wada_r[:, :, blk * 256:(blk + 1) * 256], [], wtb, "ld_wst%d" % (blk % 2))
        for j2 in range(2):
            fc = blk * 2 + j2
            bank, bb = OB[fc % 2], bf("OB%d" % (fc % 2))
            for kc in range(8):
                op(PE, I("matmul", bank[:, 0:8], lhsT=wv[:, kc, j2 * 128:(j2 + 1) * 128], rhs=siluT[:, kc, :],
                         start=(kc == 0), stop=(kc == 7)), wtb + [bf("siluT")], [bb])
            op(DVE, I("tensor_copy", out=adaT[:, fc, :], in_=bank[:, 0:8]), [], [bb, bf("adaT")])
    op(DVE, I("tensor_tensor", out=adaT[:], in0=adaT[:], in1=gpb[:, 8:24].unsqueeze(2).broadcast_to([128, 16, 8]),
              op=ALU.add), [bf("gpb1")], [bf("adaT")])
    op(DVE, I("tensor_scalar", out=adaT[:, 8:16, :], in0=adaT[:, 8:16, :], scalar1=1.0, scalar2=None, op0=ALU.add),
       [], [bf("adaT")])
    op(DVE, I("tensor_tensor", out=mod[:, 0, :, :], in0=adaT[:, 8:16, :],
              in1=gpb[:, 0:8].unsqueeze(2).broadcast_to([128, 8, 8]), op=ALU.mult), [bf("adaT"), bf("gpb0")], [bf("mod")])
    op(DVE, I("tensor_copy", out=mod[:, 1, :, :], in_=adaT[:, 0:8, :]), [bf("adaT")], [bf("mod")])
    for half in range(2):
        bank, bb = MM[half], bf("MM%d" % half)
        for k2 in range(2):
            wv = xs[:, 2:4, :].rearrange("p a (b c) -> p (a b) c", c=512)
            wtb = [bf("xs2"), bf("xs3")]
            dma(SP, wv, wada_r[:, 4 * k2:4 * k2 + 4, 2048 + half * 512:2048 + (half + 1) * 512], [], wtb, "ld_wst0")
            for kk in range(4):
                kc = 4 * k2 + kk
                op(PE, I("matmul", bank[0:8, :], lhsT=siluT[:, kc, :], rhs=wv[:, kk, :], start=(kc == 0), stop=(kc == 7)),
                   wtb + [bf("siluT")], [bb])
        op(DVE, I("tensor_tensor", out=stg[0:8, 0, half * 512:(half + 1) * 512], in0=bank[0:8, :],
                  in1=xs[0:8, 0, half * 512:(half + 1) * 512], op=ALU.add), [bf("xs0")], [bb, bf("xr0")])
    op(DVE, I("tensor_tensor", out=ggrow, in0=ggrow, in1=xs[0:8, 1, :], op=ALU.mult), [bf("xs1")], [bf("xr0")])
    dma(SP, ggd, ggrow, [bf("xr0")], [bf("ggd")], "ggd")
    def usrc(name):
        return win_bf[UI[name], :, :, 0:U_W[name]], U_W[name]

    def a_units(full):
        return ["QA", "KA", "QB", "KVB", "VA"] if full else ["KA", "KVB", "VA"]

    b_units = ["ZA", "ZB"] + ["H%d" % f_ for f_ in range(8)] + ["OUT0", "OUT1"]
    unit_seq = a_units(False) + a_units(True)
    for k in range(n_tb):
        unit_seq += a_units(k + 1 < n_tb) + b_units
    ws = {"next_load": 0, "next_use": 0, "free": list(range(NW)), "slot": {}}

    def ws_load():
        j = ws["next_load"]
        if j >= len(unit_seq) or not ws["free"]:
            return
        ws["next_load"] += 1
        src, w = usrc(unit_seq[j])
        slot = ws["free"].pop(0)
        ws["slot"][j] = slot
        assert unit_seq[j] in cv_events, unit_seq[j]
        for ev_ in cv_events[unit_seq[j]]:
            SP.wait(ev_)
        dma(SP, wbuf[:, slot, :, 0:w], src, [], [bf("wbuf%d" % slot)], "wb%d" % slot)

    def ws_get(name):
        j = ws["next_use"]
        assert unit_seq[j] == name, (unit_seq[j], name, j)
        assert j < ws["next_load"], (j, name)
        ws["next_use"] += 1
        return ws["slot"][j]

    def ws_release(slot):
        ws["free"].append(slot)
        ws_load()

    for _ in range(NW):
        ws_load()

    def hkc(hb_, kc_):
        return bf("hT%d_%d" % (hb_, kc_))

    def mmbank():
        i = cnt["mm"] % 2
        cnt["mm"] += 1
        return MM[i], bf("MM%d" % i)

    def slot_of(tile):
        return (tile + 4) % RING

    def tb_tiles(k):
        if k == -1:
            return [-2, -1]
        if k == n_tb:
            return [n_own, n_own + 1]
        return [4 * k + t for t in range(4)]

    def tb_slot(k):
        return min(max(k, 0), n_tb - 1) // 4

    xq = {}

    def stage_X(k):
        for t, tile in enumerate(tb_tiles(k)):
            xi = cnt["x"] % NX
            cnt["x"] += 1
            xq[(k, t)] = xi
            dma(ACT, xs[:, xi, :], xe[(tile + 2) * 128:(tile + 3) * 128, :], [], [bf("xs%d" % xi)], "xs%d" % xi)

    def stage_N(k):
        for t, tile in enumerate(tb_tiles(k)):
            xi = xq[(k, t)]
            xb, nb, smn = bf("xs%d" % xi), bf("xn%d" % t), bf("sm_n%d" % t)
            c0, r0 = SM_SSQ + t, SM_RSTD + t
            op(DVE, I("scalar_tensor_tensor", out=junk2[:], in0=xs[:, xi, :], scalar=1.0, in1=xs[:, xi, :], op0=ALU.mult,
                      op1=ALU.mult, accum_out=small[:, c0:c0 + 1]), [xb], [bf("lerp_tmp"), smn])
            op(POOL, I("tensor_scalar", out=small[:, c0:c0 + 1], in0=small[:, c0:c0 + 1], scalar1=1.0 / D, scalar2=1e-6,
                       op0=ALU.mult, op1=ALU.add), [], [smn])
            op(POOL, I("tensor_tensor", out=small[:, r0:r0 + 1], in0=small[:, c0:c0 + 1], in1=small[:, SM_MH:SM_MH + 1],
                       op=ALU.pow), [smc], [smn])
            op(DVE, I("tensor_scalar", out=xn[:, t, :], in0=xs[:, xi, :], scalar1=small[:, r0:r0 + 1], scalar2=None,
                      op0=ALU.mult), [xb, smn], [nb])

    def stage_T(k):
        nt = len(tb_tiles(k))
        hb = k % 2
        s = tb_slot(k)
        hbuf = bf("hT%d" % hb)
        for j in range(4):
            trb, trB = TR[j % 2], bf("TR%d" % (j % 2))
            for kc2 in range(2):
                kc = 2 * j + kc2
                for t in range(nt):
                    op(PE, I("transpose", out=trb[:, kc2 * 512 + t * 128:kc2 * 512 + (t + 1) * 128],
                             in_=xn[:, t, kc * 128:(kc + 1) * 128], identity=idb[:]), [bf("xn%d" % t), bf("idb")], [trB])
            for kc2 in range(2):
                kc = 2 * j + kc2
                op(ACT, I("activation", out=hT[:, hb, kc, 0:nt * 128], in_=trb[:, kc2 * 512:kc2 * 512 + nt * 128],
                          func=AF.Identity, scale=mod[:, 0, kc, s:s + 1], bias=mod[:, 1, kc, s:s + 1]),
                   [bf("mod")], [trB, hkc(hb, kc)])

    def stage_IP(k):
        tiles = tb_tiles(k)
        nt = len(tiles)
        ntok = nt * 128
        full = nt == 4
        hb = k % 2
        hbuf = bf("hT%d" % hb)
        s0 = slot_of(tiles[0])

        def ringb(nm):
            return [bf("%s%d" % (nm, s0 + t)) for t in range(nt)]

        def fm_unit(name, nmc, evac):
            slot = ws_get(name)
            wb = bf("wbuf%d" % slot)
            for mc in range(nmc):
                bank, bb = mmbank()
                for kc in range(8):
                    op(PE, I("matmul", bank[:, 0:ntok], lhsT=wbuf[:, slot, kc, mc * 128:(mc + 1) * 128],
                             rhs=hT[:, hb, kc, 0:ntok], start=(kc == 0), stop=(kc == 7)), [wb, hkc(hb, kc)], [bb])
                evac(mc, bank, bb)
            return slot

        if full:
            ws_release(fm_unit("QA", 4, lambda mc, bank, bb: op(ACT, I("mul", out=QA[:, hb, mc, :], in_=bank[:, :], mul=0.125),
                                                                [], [bb, bf("QA%d" % hb)])))
        ws_release(fm_unit("KA", 4, lambda mc, bank, bb: op(ACT, I("copy", out=KA[:, s0:s0 + nt, mc, :],
                                                                   in_=bank[:, 0:ntok].rearrange("p (a b) -> p a b", b=128)),
                                                            [], [bb] + ringb("KA"))))
        if full:
            ws_release(fm_unit("QB", 4, lambda mc, bank, bb: op(ACT, I("mul", out=QB[:, hb, mc, :], in_=bank[:, :], mul=0.125),
                                                                [], [bb, bf("QB%d" % hb)])))
        slot = fm_unit("KVB", 1, lambda mc, bank, bb: op(ACT, I("copy", out=KB[:, s0:s0 + nt, :],
                                                                in_=bank[:, 0:ntok].rearrange("p (a b) -> p a b", b=128)),
                                                         [], [bb] + ringb("KB")))
        wb = bf("wbuf%d" % slot)
        bank, bb = mmbank()
        for t in range(nt):
            for kc in range(8):
                op(PE, I("matmul", bank[:, t * 128:(t + 1) * 128], lhsT=hT[:, hb, kc, t * 128:(t + 1) * 128],
                         rhs=wbuf[:, slot, kc, 128:256], start=(kc == 0), stop=(kc == 7)), [wb, hkc(hb, kc)], [bb])
        op(DVE, I("tensor_copy", out=VB[:, s0:s0 + nt, :, 0:64],
                  in_=bank[:, 0:ntok].rearrange("p (a b c) -> p a b c", b=2, c=64)), [], [bb] + ringb("VB"))
        ws_release(slot)
        slot = ws_get("VA")
        wb = bf("wbuf%d" % slot)
        for t in range(nt):
            bank, bb = mmbank()
            for kc in range(8):
                op(PE, I("matmul", bank[:, :], lhsT=hT[:, hb, kc, t * 128:(t + 1) * 128], rhs=wbuf[:, slot, kc, :],
                         start=(kc == 0), stop=(kc == 7)), [wb, hkc(hb, kc)], [bb])
            op(DVE, I("tensor_copy", out=VA[:, s0 + t, :, 0:64], in_=bank[:, :].rearrange("p (a b) -> p a b", b=64)),
               [], [bb, bf("VA%d" % (s0 + t))])
        ws_release(slot)

    def tnb_slot():
        i = cnt["tnb"] % 4
        cnt["tnb"] += 1
        return i

    def blend_mul(src_ap, srcbuf, fcol):
        i = tnb_slot()
        ncol = fcol - SM_F + SM_OMFN if fcol < SM_OMF else fcol - SM_OMF + SM_FN
        op(DVE, I("tensor_scalar", out=TNB[:, i, :], in0=src_ap, scalar1=small[:, ncol:ncol + 1], scalar2=None,
                  op0=ALU.add), [srcbuf, smc, bf("sm_f")], [bf("TNB%d" % i)])
        return ("TNB", i)

    def blend_lerp(t_int, t_full, b):
        i = tnb_slot()
        op(DVE, I("tensor_scalar", out=TNB[:, i, :], in0=TN[:, t_full, :], scalar1=small[:, SM_F + b:SM_F + b + 1],
                  scalar2=None, op0=ALU.mult), [bf("TN"), smc, bf("sm_f")], [bf("TNB%d" % i)])
        op(DVE, I("tensor_scalar", out=lerp_tmp[:], in0=TN[:, t_int, :], scalar1=small[:, SM_OMF + b:SM_OMF + b + 1],
                  scalar2=None, op0=ALU.mult), [bf("TN"), smc], [bf("lerp_tmp")])
        op(DVE, I("tensor_tensor", out=TNB[:, i, :], in0=TNB[:, i, :], in1=lerp_tmp[:], op=ALU.add),
           [bf("lerp_tmp")], [bf("TNB%d" % i)])
        return ("TNB", i)

    def na_tables(i):
        bi, b = i % 16, i // 16
        tnb = bf("TN")

        def st(o):
            return ("TN", {-2: T_IM2, -1: T_F[-1], 0: T_F[0], 1: T_F[1], 2: T_IP2}[o])
        if bi == 0:
            f, omf = SM_F + b, SM_OMF + b
            return [(-2, blend_mul(TN[:, T_IM2, :], tnb, omf)), (-1, blend_mul(TN[:, T_F[-1], :], tnb, omf)),
                    (0, st(0)), (1, st(1)), (2, blend_lerp(T_IP2, T_F[2], b)), (3, blend_mul(TN[:, T_F[3], :], tnb, f))]
        if bi == 1:
            omf = SM_OMF + b
            return [(-2, blend_mul(TN[:, T_IM2, :], tnb, omf)), (-1, st(-1)), (0, st(0)), (1, st(1)),
                    (2, blend_lerp(T_IP2, T_F[2], b))]
        if bi == 15:
            f, omf = SM_F + b + 1, SM_OMF + b + 1
            return [(-3, blend_mul(TN[:, T_F[-3], :], tnb, f)), (-2, blend_lerp(T_IM2, T_F[-2], b + 1)),
                    (-1, st(-1)), (0, st(0)), (1, blend_mul(TN[:, T_F[1], :], tnb, omf)),
                    (2, blend_mul(TN[:, T_IP2, :], tnb, omf))]
        if bi == 14:
            omf = SM_OMF + b + 1
            return [(-2, blend_lerp(T_IM2, T_F[-2], b + 1)), (-1, st(-1)), (0, st(0)), (1, st(1)),
                    (2, blend_mul(TN[:, T_IP2, :], tnb, omf))]
        return [(o, st(o)) for o in (-2, -1, 0, 1, 2)]

    def sw_tables(i):
        bi, b = i % 16, i // 16
        res = []
        for o in (-1, 0, 1):
            if o == -1 and bi == 0:
                res.append((o, blend_mul(TS[:, 0, :], bf("TS"), SM_OMF + b)))
            elif o == 1 and bi == 15:
                res.append((o, blend_mul(TS[:, 2, :], bf("TS"), SM_OMF + b + 1)))
            else:
                res.append((o, ("TS", o + 1)))
        return res

    def tbl_ap(ref, half):
        kind, idx = ref
        src = {"TN": TN, "TNB": TNB, "TS": TS}[kind]
        return src[:, idx, half * 512:(half + 1) * 512], bf(kind if kind != "TNB" else "TNB%d" % idx)

    def stage_B(k):
        hb = k % 2
        hbuf = bf("hT%d" % hb)
        s = k // 4
        if k == 0:
            late_convs(1)
        if k % 4 == 0:
            dma(SP, GG[:], ggd[s:s + 1, :].broadcast_to([128, D]), [bf("ggd")], [bf("GG")], "ld_GG")
        for zi, name in enumerate(("ZA", "ZB")):
            slot = ws_get(name)
            wb = bf("wbuf%d" % slot)
            for t in range(4):
                bank, bb = mmbank()
                for kc in range(8):
                    op(PE, I("matmul", bank[:, :], lhsT=hT[:, hb, kc, t * 128:(t + 1) * 128], rhs=wbuf[:, slot, kc, :],
                             start=(kc == 0), stop=(kc == 7)), [wb, hkc(hb, kc)], [bb])
                op(ACT, I("activation", out=sz[:, t, zi * 512:(zi + 1) * 512], in_=bank[:, :], func=AF.Silu),
                   [], [bb, bf("sz%d" % t)])
            ws_release(slot)

        for t in range(4):
            i = 4 * k + t
            if k == 0 and t >= 1:
                late_convs(t + 1)

            def emit_st_pair(kind, o, ref):
                ks = slot_of(i + o)
                banks = []
                for hf in range(2):
                    sti = cnt["st"] % 4
                    cnt["st"] += 1
                    banks.append(STR[sti])
                for hf in range(2):
                    bank, bb = banks[hf]
                    tap, tbuf = tbl_ap(ref, hf)
                    op(PE, I("matmul", bank[:, :], lhsT=idb[:], rhs=tap, start=True, stop=False, skip_group_check=True),
                       [bf("idb"), tbuf], [bb])
                if kind == "na":
                    for hh in range(4):
                        for hf in range(2):
                            bank, bb = banks[hf]
                            op(PE, I("matmul", bank[:, hh * 128:(hh + 1) * 128], lhsT=KA[hf * 64:(hf + 1) * 64, ks, hh, :],
                                     rhs=QA[hf * 64:(hf + 1) * 64, hb, hh, t * 128:(t + 1) * 128], start=False, stop=(hh == 3),
                                     skip_group_check=True), [bf("KA%d" % ks), bf("QA%d" % hb)], [bb])
                else:
                    for hf in range(2):
                        bank, bb = banks[hf]
                        op(PE, I("matmul", bank[:, :], lhsT=KB[hf * 64:(hf + 1) * 64, ks, :],
                                 rhs=QB[hf * 64:(hf + 1) * 64, hb, :, t * 128:(t + 1) * 128], start=False, stop=True,
                                 skip_group_check=True), [bf("KB%d" % ks), bf("QB%d" % hb)], [bb])
                pis = []
                for hf in range(2):
                    bank, bb = banks[hf]
                    pi = cnt["pt"] % NPT
                    cnt["pt"] += 1
                    op(ACT, I("activation", out=PT[:, pi, :], in_=bank[:, :], func=AF.Exp), [], [bb, bf("PT%d" % pi)])
                    pis.append(pi)
                return pis

            def emit_pv(step, pi, first, last):
                kind, o, hf, ref = step
                ks = slot_of(i + o)
                obb = bf("OB%d" % hf)
                ov = OB[hf][:, 0:260].rearrange("p (h e) -> p h e", e=65)
                for hh in range(4):
                    if kind == "na":
                        rhs, rb = VA[:, ks, 2 * hh + hf, :], bf("VA%d" % ks)
                    else:
                        rhs, rb = VB[:, ks, hf, :], bf("VB%d" % ks)
                    op(PE, I("matmul", ov[:, hh, :], lhsT=PT[:, pi, hh * 128:(hh + 1) * 128], rhs=rhs,
                             start=(first and hh == 0), stop=(last and hh == 3), skip_group_check=True),
                       [bf("PT%d" % pi), rb], [obb])

            def finalize(kind):
                oi = 0 if kind == "na" else 1
                for hf in range(2):
                    obb, smr = bf("OB%d" % hf), bf("sm_rd%d" % hf)
                    ov = OB[hf][:, 0:260].rearrange("p (h e) -> p h e", e=65)
                    rd = small[:, SM_RD + hf * 4:SM_RD + hf * 4 + 4]
                    if kind == "na":
                        op(DVE, I("reciprocal", out=rd, in_=ov[:, :, 64]), [], [obb, smr])
                    else:
                        op(DVE, I("tensor_tensor", out=rd, in0=ov[:, :, 64], in1=small[:, SM_ES + 4 * hf:SM_ES + 4 * hf + 4],
                                  op=ALU.add), [bf("sm_es")], [obb, smr])
                        op(DVE, I("reciprocal", out=rd, in_=rd), [], [smr])
                    tv = otmp[:, hf, :].rearrange("p (h d) -> p h d", d=64)
                    op(DVE, I("tensor_tensor", out=tv, in0=ov[:, :, 0:64], in1=rd.unsqueeze(2).broadcast_to([128, 4, 64]),
                              op=ALU.mult), [smr], [obb, bf("otmp%d" % hf)])
                    if kind == "na":
                        dst = oab[:, t % 2, 0, :].rearrange("p (hh two d) -> p hh two d", two=2, d=64)[:, :, hf, :]
                        zz = sz[:, t, 0:512].rearrange("p (hh two d) -> p hh two d", two=2, d=64)[:, :, hf, :]
                    else:
                        dst = oab[:, t % 2, 1, hf * 256:(hf + 1) * 256].rearrange("p (h d) -> p h d", d=64)
                        zz = sz[:, t, 512 + hf * 256:512 + (hf + 1) * 256].rearrange("p (h d) -> p h d", d=64)
                    op(POOL, I("tensor_tensor", out=dst, in0=tv, in1=zz, op=ALU.mult),
                       [bf("otmp%d" % hf), bf("sz%d" % t)], [bf("oab%d_%d" % (t % 2, oi))])

            def run_steps(kind, tables):
                n = 2 * len(tables)
                pend = []
                started = set()

                def flush_one():
                    pj, pst, ppi = pend.pop(0)
                    emit_pv(pst, ppi, pst[2] not in started, pj >= n - 2)
                    started.add(pst[2])
                for jo, (o, ref) in enumerate(tables):
                    pis = emit_st_pair(kind, o, ref)
                    for hf in range(2):
                        pend.append((2 * jo + hf, (kind, o, hf, ref), pis[hf]))
                    while len(pend) > 2:
                        flush_one()
                while pend:
                    flush_one()
                finalize(kind)

            def emit_tr(tt):
                trb, trB = TR[tt % 2], bf("TR%d" % (tt % 2))
                for j in range(8):
                    op(PE, I("transpose", out=trb[:, j * 128:(j + 1) * 128],
                             in_=oab[:, tt % 2, j // 4, (j % 4) * 128:(j % 4 + 1) * 128], identity=idb[:]),
                       [bf("oab%d_%d" % (tt % 2, j // 4)), bf("idb")], [trB])
                op(DVE, I("tensor_copy", out=oT[:, :, tt * 128:(tt + 1) * 128],
                          in_=trb[:, :].rearrange("p (a b) -> p a b", b=128)), [], [trB, bf("oT")])

            run_steps("na", na_tables(i))
            if t > 0:
                emit_tr(t - 1)
            run_steps("sw", sw_tables(i))
            if k == 0:
                dump("oab%d" % t, oab[:, t % 2], [bf("oab%d_0" % (t % 2)), bf("oab%d_1" % (t % 2))])
                dump("sz%d" % t, sz[:, t], [bf("sz%d" % t)])

        for fc in range(8):
            sh = ws_get("H%d" % fc)
            whb = bf("wbuf%d" % sh)
            for gi in range(2):
                bank, bb = ST[gi], bf("ST%d" % gi)
                for kc in range(8):
                    op(PE, I("matmul", bank[:, :], lhsT=wbuf[:, sh, kc, gi * 128:(gi + 1) * 128], rhs=hT[:, hb, kc, :],
                             start=(kc == 0), stop=(kc == 7)), [whb, hkc(hb, kc)], [bb])
                op(ACT, I("activation", out=sg[:, gi, :], in_=bank[:, :], func=AF.Sigmoid), [], [bb, bf("sg%d" % gi)])
            if fc == 0:
                emit_tr(3)
            for pi_ in range(2):
                bank, bb = MM[pi_], bf("MM%d" % pi_)
                for c in range(4):
                    op(PE, I("matmul", bank[:, :], lhsT=wbuf[:, sh, pi_ * 4 + c, 256:384], rhs=oT[:, pi_ * 4 + c, :],
                             start=(c == 0), stop=(c == 3)), [whb, bf("oT")], [bb])
                op(DVE, I("tensor_tensor", out=m12[:, pi_, :], in0=bank[:, :], in1=sg[:, pi_, :], op=ALU.mult),
                   [bf("sg%d" % pi_)], [bb, bf("m12_%d" % pi_)])
            op(POOL, I("tensor_tensor", out=mT[:, fc, :], in0=m12[:, 0, :], in1=m12[:, 1, :], op=ALU.add),
               [bf("m12_0"), bf("m12_1")], [bf("mT")])
            ws_release(sh)

        if k == 0:
            dump("oT", oT[:], [bf("oT")])
            dump("mT", mT[:], [bf("mT")])
            dump("GG", GG[:], [bf("GG")])
        so = [ws_get("OUT0"), ws_get("OUT1")]
        ybanks = [[(ST[0], bf("ST0")), (ST[1], bf("ST1"))], [(OB[0], bf("OB0")), (OB[1], bf("OB1"))],
                  [(MM[0], bf("MM0")), (MM[1], bf("MM1"))]]
        for t in range(4):
            i = 4 * k + t
            xi = cnt["xr"] % 2
            cnt["xr"] += 1
            if t == 3 and k + 2 <= n_tb:
                stage_T(k + 2)
            smy = bf("sm_y%d" % (t % 2))
            SY = SM_Y + 4 * (t % 2)
            xrb = bf("xr%d" % xi)
            dma(POOL, xr[:, xi, :], xe[(i + 2) * 128:(i + 3) * 128, :], [], [xrb], "xr%d" % xi)
            for half in range(2):
                bank, bb = ybanks[t % 2][half]
                for kc in range(8):
                    op(PE, I("matmul", bank[:, :], lhsT=mT[:, kc, t * 128:(t + 1) * 128], rhs=wbuf[:, so[half], kc, :],
                             start=(kc == 0), stop=(kc == 7)), [bf("wbuf%d" % so[half]), bf("mT")], [bb])
                op(ACT, I("activation", out=junk[:, 0:512], in_=bank[:, :], func=AF.Square,
                          accum_out=small[:, SY + half:SY + half + 1]), [], [bb, bf("junk"), smy])
            op(POOL, I("tensor_tensor", out=small[:, SY + 2:SY + 3], in0=small[:, SY:SY + 1],
                       in1=small[:, SY + 1:SY + 2], op=ALU.add), [], [smy])
            op(POOL, I("tensor_scalar", out=small[:, SY + 2:SY + 3], in0=small[:, SY + 2:SY + 3], scalar1=1.0 / D,
                       scalar2=1e-6, op0=ALU.mult, op1=ALU.add), [], [smy])
            op(POOL, I("tensor_tensor", out=small[:, SY + 3:SY + 4], in0=small[:, SY + 2:SY + 3],
                       in1=small[:, SM_MH:SM_MH + 1], op=ALU.pow), [smc], [smy])
            for half in range(2):
                bank, bb = ybanks[t % 2][half]
                op(DVE, I("scalar_tensor_tensor", out=t1[:, half * 512:(half + 1) * 512], in0=bank[:, :],
                          scalar=small[:, SY + 3:SY + 4], in1=GG[:, half * 512:(half + 1) * 512], op0=ALU.mult,
                          op1=ALU.mult), [smy, bf("GG")], [bb, bf("t1")])
            op(POOL, I("tensor_tensor", out=xr[:, xi, :], in0=xr[:, xi, :], in1=t1[:], op=ALU.add), [bf("t1")], [xrb])
            dma(POOL, y[i * 128:(i + 1) * 128, :], xr[:, xi, :], [xrb], [bf("yout")], "yst%d" % xi)
        ws_release(so[0])
        ws_release(so[1])

    def setup_tables():
        ld(t1[:], cmask, bf("t1"))
        for o in range(-3, 4):
            sl = (o + 3) % 2
            sbuf_ = bf("xr%d" % sl)
            ld(stg[:, sl, :], rpbx[:, o + 3, :], sbuf_)
            op(DVE, I("tensor_tensor", out=TN[:, T_F[o], :], in0=stg[:, sl, :], in1=t1[:], op=ALU.add),
               [sbuf_, bf("t1")], [bf("TN")])
        for j, (o, ti) in enumerate(((-2, T_IM2), (2, T_IP2))):
            sbuf_ = bf("xr%d" % j)
            ld(stg[:, j, :], rmask[:, j, :], sbuf_)
            op(DVE, I("tensor_tensor", out=TN[:, ti, :], in0=TN[:, T_F[o], :], in1=stg[:, j, :], op=ALU.add),
               [sbuf_], [bf("TN")])
        ld(stg[:, 0, 0:384], swd.rearrange("p a b -> p (a b)"), bf("xr0"))
        xr0v = t1[:].rearrange("p (a b) -> p a b", b=128)
        for o in range(3):
            for h in range(8):
                slope = float(2.0 ** (-(h + 1)))
                op(ACT, I("activation", out=xr0v[:, h, :], in_=stg[:, 0, o * 128:(o + 1) * 128], func=AF.Copy, scale=-slope),
                   [bf("xr0")], [bf("t1")])
            ld(stg[:, 1, :], swm[:, o, :], bf("xr1"))
            op(DVE, I("tensor_tensor", out=TS[:, o, :], in0=t1[:], in1=stg[:, 1, :], op=ALU.add),
               [bf("t1"), bf("xr1")], [bf("TS")])


    def late_convs(stage):
        if stage == 0:
            conv_simple("ZA", C_ZA)
            conv_simple("ZB", C_ZB)
            conv_h(0)
            conv_h(1)
        elif stage == 1:
            conv_h(2)
            conv_h(3)
        elif stage in (2, 3):
            conv_h(2 * stage)
            conv_h(2 * stage + 1)
        elif stage == 4:
            for half in range(2):
                conv(win_bf[UI["OUT%d" % half], :, :, :], wout_r[:, :, half * 512:(half + 1) * 512], "OUT%d" % half)

    stage_X(-1)
    stage_N(-1)
    stage_X(0)
    stage_T(-1)
    stage_IP(-1)
    stage_N(0)
    late_convs(0)
    stage_X(1 if n_tb > 1 else n_tb)
    stage_T(0)
    dump("hT", hT[:, 0], [hkc(0, kc_) for kc_ in range(8)])
    dump("mod", mod[:], [bf("mod")])
    stage_IP(0)
    dump("QA", QA[:, 0], [bf("QA0")])
    dump("KA", KA[:, 4:8], [bf("KA%d" % s_) for s_ in range(4, 8)])
    dump("VA", VA[:, 4:8], [bf("VA%d" % s_) for s_ in range(4, 8)])
    dump("QB", QB[:, 0], [bf("QB0")])
    dump("KB", KB[:, 4:8], [bf("KB%d" % s_) for s_ in range(4, 8)])
    dump("VB", VB[:, 4:8], [bf("VB%d" % s_) for s_ in range(4, 8)])
    stage_N(1 if n_tb > 1 else n_tb)
    setup_tables()
    stage_T(1)
    for k in range(n_tb):
        if k + 2 <= n_tb:
            stage_X(k + 2)
        stage_IP(k + 1)
        if k + 2 <= n_tb:
            stage_N(k + 2)
        stage_B(k)
    for nm in ("yst0", "yst1"):
        s_ = dsem(nm)
        if s_.v:
            POOL.wait((s_, s_.v))

    with nc.Block() as block:
        def emit(engobj):
            def body(e):
                for item in engobj.prog:
                    if item[0] == "w":
                        e.wait_ge(item[1], item[2])
                    else:
                        item[1](e).then_inc(item[2], item[3])
            return body
        block.tensor(emit(PE))
        block.scalar(emit(ACT))
        block.vector(emit(DVE))
        block.gpsimd(emit(POOL))
        block.sync(emit(SP))
    es.close()
    return nc


def _const_tables():
    GW, KW = 64, 16
    col = np.arange(GW)
    cstart = np.clip(col - KW // 2, 0, GW - KW)
    ki = np.arange(128)
    kr, kc = ki // 64, ki % 64
    qr, qc = ki // 64, ki % 64
    cval = ((kc[:, None] >= cstart[qc][None, :]) & (kc[:, None] < cstart[qc][None, :] + KW)).astype(np.float32)
    dc = np.clip(kc[:, None] - qc[None, :] + KW - 1, 0, 2 * KW - 2)
    dr_rel = kr[:, None] - qr[None, :]
    dr_idx = {}
    rmask = {}
    for o in range(-3, 4):
        dr = 2 * o + dr_rel
        dr_idx[o] = np.clip(dr + 7, 0, 14)
        rmask[o] = ((dr >= -4) & (dr <= 3)).astype(np.float32)
    k = np.arange(128)
    swd = np.stack([np.abs(128 * o + k[:, None] - k[None, :]) for o in (-1, 0, 1)], 1).astype(np.float32)
    swm = (swd <= 128).astype(np.float32)
    return cval, dc, dr_idx, rmask, swd, swm


def prep_inputs(X, c_seq, seq_len, n_tb, ncores, weights):
    (w_ada, b_ada, g_pre, g_post, w_in, na_rpb, sw_sink, w_pa, w_pb, w_out) = weights
    tok_per_core = n_tb * 512
    ntot = X.shape[0]
    assert tok_per_core * ncores == ntot
    cval, dc, dr_idx, rmask, swd, swm = _const_tables()
    rpbx = np.zeros((128, 7, 8, 128), np.float32)
    for o in range(-3, 4):
        for hi, h in enumerate(HEAD_ORDER):
            rpbx[:, o + 3, hi, :] = na_rpb[h][dr_idx[o], dc]
    rpbx = rpbx.reshape(128, 7, 1024)
    NEGH = 30000.0
    cmask = np.ascontiguousarray(np.broadcast_to((cval[:, None, :] - 1.0) * NEGH, (128, 8, 128))).reshape(128, 1024).astype(np.float32)
    rm = np.stack([np.broadcast_to((rmask[-2][:, None, :] - 1.0) * NEGH, (128, 8, 128)).reshape(128, 1024),
                   np.broadcast_to((rmask[2][:, None, :] - 1.0) * NEGH, (128, 8, 128)).reshape(128, 1024)], 1).astype(np.float32)
    swm8 = np.ascontiguousarray(np.broadcast_to((swm[:, :, None, :] - 1.0) * NEGH, (128, 3, 8, 128))).reshape(128, 3, 1024).astype(np.float32)
    sel = np.zeros((8, 8, 128), np.float32)
    for s in range(8):
        sel[s, s, :] = 1.0
    sel = sel.reshape(8, 1024)
    common = {
        "w_ada": np.ascontiguousarray(w_ada), "w_in": np.ascontiguousarray(w_in),
        "w_pa": np.ascontiguousarray(w_pa), "w_pb": np.ascontiguousarray(w_pb), "w_out": np.ascontiguousarray(w_out),
        "b_adaT": np.ascontiguousarray(b_ada[:2048].reshape(16, 128).T),
        "b_gate8": np.ascontiguousarray(np.broadcast_to(b_ada[2048:][None, :], (8, D))),
        "g_preT": np.ascontiguousarray(g_pre.reshape(8, 128).T),
        "g_post8": np.ascontiguousarray(np.broadcast_to(g_post[None, :], (8, D))),
        "sink": np.ascontiguousarray(np.broadcast_to(sw_sink[None, :], (128, 8))),
        "rpbx": rpbx, "cmask": cmask, "rmask": np.ascontiguousarray(rm),
        "swd": np.ascontiguousarray(swd), "swm": swm8, "ident": np.eye(128, dtype=np.float32),
    }
    in_maps = []
    for c in range(ncores):
        start = c * tok_per_core
        xe = np.zeros((tok_per_core + 512, D), np.float32)
        lo, hi = start - 256, start + tok_per_core + 256
        slo, shi = max(lo, 0), min(hi, ntot)
        xe[slo - lo:shi - lo] = X[slo:shi]
        csT = np.zeros((128, 8, 8), np.float32)
        flg = np.zeros((128, 8), np.float32)
        for s in range(n_tb // 4):
            seq = (start + 2048 * s) // seq_len
            csT[:, :, s] = c_seq[seq].reshape(8, 128).T
        for b in range(n_tb // 4 + 1):
            flg[:, b] = 1.0 if (start + 2048 * b) % seq_len == 0 else 0.0
        m = dict(common)
        m["xe"] = xe
        m["csT"] = csT
        m["flg"] = flg
        in_maps.append(m)
    return in_maps


_NC_CACHE = {}
_LAST = None


def run(X, c_seq, seq_len, n_tb, ncores, weights):
    in_maps = prep_inputs(X, c_seq, seq_len, n_tb, ncores, weights)
    if n_tb not in _NC_CACHE:
        _NC_CACHE[n_tb] = build(n_tb)
    res = run_bass_kernel_spmd(_NC_CACHE[n_tb], in_maps, core_ids=list(range(ncores)))
    global _LAST
    _LAST = res.results
    return np.concatenate([r["y"] for r in res.results], axis=0)


def kernel(x_prompt, x_sample, c_prompt, c_sample, w_ada, b_ada, g_pre, g_post, w_in, na_rpb, sw_sink,
           w_pa, w_pb, w_out):
    f = lambda a: np.asarray(a, dtype=np.float32)
    xp, xs_ = f(x_prompt), f(x_sample)
    X = np.concatenate([xp.reshape(-1, D), xs_.reshape(-1, D)], axis=0)
    c_seq = np.concatenate([f(c_prompt), f(c_sample)], axis=0)
    weights = (f(w_ada)[0], f(b_ada)[0], f(g_pre)[0], f(g_post)[0], f(w_in)[0], f(na_rpb)[0], f(sw_sink)[0],
               f(w_pa)[0], f(w_pb)[0], f(w_out)[0])
    Y = run(X, c_seq, 8192, 20, NCORES, weights)
    n_p = xp.shape[0] * xp.shape[1]
    return (Y[:n_p].reshape(xp.shape).astype(np.float32), Y[n_p:].reshape(xs_.shape).astype(np.float32))
```
